# Optimizing a Trainium2 kernel written in Bass

```python
import math
import jax, jax.numpy as jnp
from jax import lax
import numpy as np

D_MODEL = 2048
BATCH = 8
SEQ = 4096
DEPTH = 4

HEAD_DIM = 128
V_DIM = 2 * HEAD_DIM
N_HEADS = D_MODEL // (2 * HEAD_DIM)
ATTN_WIDTH = N_HEADS * V_DIM
QK_WIDTH = N_HEADS * 2 * HEAD_DIM
Q_BLOCK = 128
ROPE_THETA = 10000.0
D_RNN = D_MODEL
N_RG_BLOCKS = 8
RG_BLOCK = D_RNN // N_RG_BLOCKS
CONV_WIDTH = 4
LRU_C = 8.0
D_FF = -(-8 * D_MODEL // (3 * 256)) * 256
N_IN = 2 * QK_WIDTH + ATTN_WIDTH + 2 * D_RNN + 2 * D_MODEL
LN_EPS = 1e-5
DEEPNORM_ALPHA = (2.0 * DEPTH) ** 0.25
DEEPNORM_BETA = (8.0 * DEPTH) ** -0.25

kernel_name = "hybrid_diffattn_rglru_gated_deepnorm"


def layer_norm(x, g, b):
    xf = x.astype(jnp.float32)
    mu = jnp.mean(xf, axis=-1, keepdims=True)
    var = jnp.mean(jnp.square(xf - mu), axis=-1, keepdims=True)
    y = (xf - mu) * lax.rsqrt(var + LN_EPS) * g.astype(jnp.float32) + b.astype(jnp.float32)
    return y.astype(x.dtype)


def rope_tables(seq):
    half = HEAD_DIM // 2
    inv_freq = ROPE_THETA ** (-jnp.arange(half, dtype=jnp.float32) * 2.0 / HEAD_DIM)
    ang = jnp.arange(seq, dtype=jnp.float32)[:, None] * inv_freq[None, :]
    ang = jnp.concatenate([ang, ang], axis=-1)
    return jnp.cos(ang), jnp.sin(ang)


def apply_rope(t, cos, sin):
    half = HEAD_DIM // 2
    t1, t2 = t[..., :half], t[..., half:]
    rot = jnp.concatenate([-t2, t1], axis=-1)
    c = cos[None, :, None, None, :]
    s = sin[None, :, None, None, :]
    return t * c + rot * s


def diff_attention(q, k, v, lam):
    B, S = q.shape[0], q.shape[1]
    nb = S // Q_BLOCK
    qb = q.reshape(B, nb, Q_BLOCK, N_HEADS, 2, HEAD_DIM).transpose(1, 0, 2, 3, 4, 5)
    kpos = jnp.arange(S)
    vf = v.astype(jnp.float32)
    scale = HEAD_DIM ** -0.5

    def one_block(args):
        qblk, bi = args
        s = jnp.einsum('bqhcd,bkhcd->bhcqk', qblk, k).astype(jnp.float32) * scale
        qpos = bi * Q_BLOCK + jnp.arange(Q_BLOCK)
        causal = qpos[:, None] >= kpos[None, :]
        s = jnp.where(causal, s, -jnp.inf)
        p = jax.nn.softmax(s, axis=-1)
        a = p[:, :, 0] - lam * p[:, :, 1]
        return jnp.einsum('bhqk,bkhe->bqhe', a, vf)

    o = lax.map(one_block, (qb, jnp.arange(nb)))
    return o.transpose(1, 0, 2, 3, 4).reshape(B, S, N_HEADS, V_DIM)


def causal_depthwise_conv(x, w, b):
    C = x.shape[-1]
    y = lax.conv_general_dilated(
        x, w.astype(x.dtype)[:, None, :], window_strides=(1,),
        padding=[(CONV_WIDTH - 1, 0)], dimension_numbers=('NWC', 'WIO', 'NWC'),
        feature_group_count=C)
    return y + b.astype(x.dtype)


def rg_lru(xc, w_rg, b_rg, lru_lambda):
    B, S, _ = xc.shape
    xf = xc.astype(jnp.float32)
    xb = xf.reshape(B, S, N_RG_BLOCKS, RG_BLOCK)
    gates = jnp.einsum('bsnc,gncd->gbsnd', xb, w_rg.astype(jnp.float32)).reshape(2, B, S, D_RNN)
    gates = gates + b_rg.astype(jnp.float32)[:, None, None, :]
    r = jax.nn.sigmoid(gates[0])
    i = jax.nn.sigmoid(gates[1])
    log_a = -LRU_C * r * jax.nn.softplus(-lru_lambda.astype(jnp.float32))
    a = jnp.exp(log_a)
    mult = jnp.sqrt(-jnp.expm1(2.0 * log_a))
    u = mult * (i * xf)

    def combine(left, right):
        a1, b1 = left
        a2, b2 = right
        return a1 * a2, a2 * b1 + b2

    _, h = lax.associative_scan(combine, (a, u), axis=1)
    return h


def setup_inputs(seed: int = 0) -> dict:
    key = jax.random.key(seed)
    ks = jax.random.split(key, 18)
    f32 = jnp.float32
    x = jax.random.normal(ks[0], (BATCH, SEQ, D_MODEL), f32)
    w_in = jax.random.normal(ks[1], (DEPTH, D_MODEL, N_IN), f32) * D_MODEL ** -0.5
    b_merge = 0.1 * jax.random.normal(ks[2], (DEPTH, 2, D_MODEL), f32)
    diff_lambda = 0.1 * jax.random.normal(ks[3], (DEPTH, 4, HEAD_DIM), f32)
    subln_g = 1.0 + 0.02 * jax.random.normal(ks[4], (DEPTH, V_DIM), f32)
    conv_w = jax.random.normal(ks[5], (DEPTH, CONV_WIDTH, D_RNN), f32) * CONV_WIDTH ** -0.5
    conv_b = 0.02 * jax.random.normal(ks[6], (DEPTH, D_RNN), f32)
    w_rg = jax.random.normal(ks[7], (DEPTH, 2, N_RG_BLOCKS, RG_BLOCK, RG_BLOCK), f32) * RG_BLOCK ** -0.5
    b_rg = 0.1 * jax.random.normal(ks[8], (DEPTH, 2, D_RNN), f32)
    u = jax.random.uniform(ks[9], (DEPTH, D_RNN), f32, minval=0.9, maxval=0.999)
    a0 = u ** (1.0 / LRU_C)
    lru_lambda = jnp.log(a0) - jnp.log1p(-a0)
    w_branch = jax.random.normal(ks[10], (DEPTH, 2, ATTN_WIDTH, D_MODEL), f32) * (ATTN_WIDTH ** -0.5 * DEEPNORM_BETA)
    w_out = jax.random.normal(ks[11], (DEPTH, D_MODEL, D_MODEL), f32) * (D_MODEL ** -0.5 * DEEPNORM_BETA)
    ln_g = 1.0 + 0.02 * jax.random.normal(ks[12], (DEPTH, 2, D_MODEL), f32)
    ln_b = 0.02 * jax.random.normal(ks[13], (DEPTH, 2, D_MODEL), f32)
    w_gate = jax.random.normal(ks[14], (DEPTH, D_MODEL, D_FF), f32) * D_MODEL ** -0.5
    w_up = jax.random.normal(ks[15], (DEPTH, D_MODEL, D_FF), f32) * (D_MODEL ** -0.5 * DEEPNORM_BETA)
    w_gate_up = jnp.concatenate([w_gate, w_up], axis=-1)
    w_down = jax.random.normal(ks[16], (DEPTH, D_FF, D_MODEL), f32) * (D_FF ** -0.5 * DEEPNORM_BETA)
    return {"x": x, "w_in": w_in, "b_merge": b_merge, "diff_lambda": diff_lambda,
            "subln_g": subln_g, "conv_w": conv_w, "conv_b": conv_b, "w_rg": w_rg,
            "b_rg": b_rg, "lru_lambda": lru_lambda, "w_branch": w_branch, "w_out": w_out,
            "ln_g": ln_g, "ln_b": ln_b, "w_gate_up": w_gate_up, "w_down": w_down}


def reference(x, w_in, b_merge, diff_lambda, subln_g, conv_w, conv_b, w_rg, b_rg,
              lru_lambda, w_branch, w_out, ln_g, ln_b, w_gate_up, w_down):
    B, S, D = x.shape
    cos, sin = rope_tables(S)
    o1 = 2 * QK_WIDTH
    o2 = o1 + ATTN_WIDTH
    o3 = o2 + D_RNN
    o4 = o3 + D_RNN
    for l in range(DEPTH):
        lam_init = 0.8 - 0.6 * math.exp(-0.3 * l)
        p = jnp.einsum('bsd,dn->bsn', x, w_in[l])
        q = p[..., :QK_WIDTH].reshape(B, S, N_HEADS, 2, HEAD_DIM)
        k = p[..., QK_WIDTH:o1].reshape(B, S, N_HEADS, 2, HEAD_DIM)
        v = p[..., o1:o2].reshape(B, S, N_HEADS, V_DIM)
        xr = p[..., o2:o3]
        gr = p[..., o3:o4]
        gm = p[..., o4:].reshape(B, S, 2, D)

        q = apply_rope(q, cos, sin)
        k = apply_rope(k, cos, sin)
        lv = diff_lambda[l].astype(jnp.float32)
        lam = jnp.exp(jnp.sum(lv[0] * lv[1])) - jnp.exp(jnp.sum(lv[2] * lv[3])) + lam_init
        o = diff_attention(q, k, v, lam)
        o = o * lax.rsqrt(jnp.mean(jnp.square(o), axis=-1, keepdims=True) + LN_EPS)
        o = (o * subln_g[l].astype(jnp.float32) * (1.0 - lam_init)).reshape(B, S, ATTN_WIDTH)
        y_attn = jnp.einsum('bse,ed->bsd', o.astype(x.dtype), w_branch[l, 0])

        xc = causal_depthwise_conv(xr, conv_w[l], conv_b[l])
        h = rg_lru(xc, w_rg[l], b_rg[l], lru_lambda[l])
        hr = (h * jax.nn.gelu(gr.astype(jnp.float32))).astype(x.dtype)
        y_rnn = jnp.einsum('bse,ed->bsd', hr, w_branch[l, 1])

        g = jax.nn.sigmoid(gm + b_merge[l])
        merged = g[:, :, 0, :] * y_attn + g[:, :, 1, :] * y_rnn
        mix = jnp.einsum('bsd,de->bse', merged, w_out[l])
        x = layer_norm(DEEPNORM_ALPHA * x + mix, ln_g[l, 0], ln_b[l, 0])

        hu = jnp.einsum('bsd,df->bsf', x, w_gate_up[l])
        act = jax.nn.silu(hu[..., :D_FF]) * hu[..., D_FF:]
        ffn = jnp.einsum('bsf,fd->bsd', act, w_down[l])
        x = layer_norm(DEEPNORM_ALPHA * x + ffn, ln_g[l, 1], ln_b[l, 1])
    return x
```

```python
import math
import os
from contextlib import ExitStack

import numpy as np
import concourse.bass as bass
import concourse.mybir as mybir
from concourse.bass_utils import run_bass_kernel_spmd

F32 = mybir.dt.float32
BF16 = mybir.dt.bfloat16
AF = mybir.ActivationFunctionType
ALU = mybir.AluOpType

D = 2048
KC = 16
HD = 128
NH = 8
DFF = 5632
FC = 44
N_IN = 14336
TB = 512
EPS = 1e-5
DEPTH = 4
ALPHA = (2.0 * DEPTH) ** 0.25
NPP = 230
O_BM, O_CW, O_CB, O_BRG, O_LAM, O_LNG, O_LNB, O_SUB, O_DL = 0, 32, 96, 112, 144, 160, 192, 224, 226
GELU_C = 2.0 * math.sqrt(2.0 / math.pi)


class _Op:
    __slots__ = ("eng", "fn", "deps", "dom", "sig", "seq", "dma", "tag")

    def __init__(self, eng, fn, deps, dom, dma):
        self.eng, self.fn, self.deps, self.dom, self.dma = eng, fn, deps, dom, dma
        self.sig = False
        self.seq = 0


class Sched:
    def __init__(self):
        self.ops = []
        self.lastw = {}
        self.readers = {}
        self.dma_last = {}
        self.tag = ""
        self.gen = {}

    def _norm(self, bufs):
        out = []
        for b in bufs:
            if isinstance(b, tuple) and len(b) == 3 and b[0] in ("ft", "bt"):
                assert self.gen[b[:2]] == b[2], ("stale temp tile", b, self.gen[b[:2]], self.tag)
                b = b[:2]
            out.append(b)
        return out

    def add(self, eng, fn, reads=(), writes=(), dma_key=None):
        idx = len(self.ops)
        deps = set()
        reads = self._norm(reads)
        writes = self._norm(writes)
        ex = [b for b in reads if isinstance(b, tuple) and b[0] == "ps"]
        if ex:
            reads = [b for b in reads if b not in ex]
            writes = list(writes) + ex
        for b in reads:
            w = self.lastw.get(b)
            if w is not None:
                deps.add(w)
        for b in writes:
            w = self.lastw.get(b)
            if w is not None:
                deps.add(w)
            rd = self.readers.get(b)
            if rd:
                deps.update(rd.values())
        dom = ("dma", dma_key) if dma_key is not None else eng
        if dma_key is not None:
            p = self.dma_last.get(dma_key)
            if p is not None:
                deps.add(p)
            self.dma_last[dma_key] = idx
        for b in reads:
            self.readers.setdefault(b, {})[dom] = idx
        for b in writes:
            self.lastw[b] = idx
            self.readers[b] = {}
        deps.discard(idx)
        if eng == "pe":
            deps = {d for d in deps if self.ops[d].eng != "pe"}
        op = _Op(eng, fn, deps, dom, dma_key is not None)
        op.tag = self.tag
        self.ops.append(op)
        return idx

    def emit(self, nc, stack):
        ops = self.ops
        for op in ops:
            if op.dma:
                op.sig = True
            for d in op.deps:
                ops[d].sig = True
        cnt = {}
        for op in ops:
            if op.sig:
                cnt[op.dom] = cnt.get(op.dom, 0) + 1
                op.seq = cnt[op.dom]
        sems = {}
        for i, dom in enumerate(cnt):
            sems[dom] = stack.enter_context(nc.semaphore("s%d" % i))
        per_eng = {}
        for i, op in enumerate(ops):
            per_eng.setdefault(op.eng, []).append(i)
        block = stack.enter_context(nc.Block())

        def run(eng_name, e):
            known = {}
            for i in per_eng.get(eng_name, []):
                op = ops[i]
                need = {}
                for d in op.deps:
                    o = ops[d]
                    if o.seq > need.get(o.dom, 0):
                        need[o.dom] = o.seq
                for dom, seq in need.items():
                    if known.get(dom, 0) >= seq:
                        continue
                    known[dom] = seq
                    unit = 16 if dom[0] == "dma" else 1
                    e.wait_ge(sems[dom], seq * unit)
                ins = op.fn(e)
                if op.sig:
                    ins.then_inc(sems[op.dom], 16 if op.dma else 1)

        @block.tensor
        def _(e):
            run("pe", e)

        @block.scalar
        def _(e):
            run("act", e)

        @block.vector
        def _(e):
            run("dve", e)

        @block.gpsimd
        def _(e):
            run("pool", e)

        @block.sync
        def _(e):
            run("sp", e)


def build_program(n_layers, S, layer0=0, dbg=False):
    NTB = S // TB
    L = n_layers
    nc = bass.Bass("TRN2", target_bir_lowering=False)
    sc = Sched()
    stack = ExitStack()

    def dram_in(name, shape, dt=F32):
        return nc.dram_tensor(name, list(shape), dt, kind="ExternalInput").ap()

    def dram_tmp(name, shape, dt):
        return nc.dram_tensor(name, list(shape), dt, kind="Internal").ap()

    xT_in = dram_in("xT", [D, S])
    w_in = dram_in("w_in", [L, D, N_IN])
    w_rg = dram_in("w_rg", [L, 2, 8, 256, 256])
    w_br = dram_in("w_branch", [L, 2, D, D])
    w_out = dram_in("w_out", [L, D, D])
    w_gu = dram_in("w_gate_up", [L, D, 2 * DFF])
    w_dn = dram_in("w_down", [L, DFF, D])
    pp_in = dram_in("pp", [128, L * NPP])
    cos_in = dram_in("cosT", [128, S])
    sin_in = dram_in("sinS", [128, S])
    perm_in = dram_in("perm", [128, 128], BF16)
    mask_in = dram_in("masks", [128, TB + 384], BF16)
    outT = nc.dram_tensor("outT", [D, S], F32, kind="ExternalOutput").ap()

    wb_in = dram_tmp("wb_in", [L, D, N_IN], BF16)
    wb_rg = dram_tmp("wb_rg", [L, 4, 128, 2048], BF16)
    wb_br = dram_tmp("wb_br", [L, 2, D, D], BF16)
    wb_out = dram_tmp("wb_out", [L, D, D], BF16)
    wb_gu = dram_tmp("wb_gu", [L, D, 2 * DFF], BF16)
    wb_dn = dram_tmp("wb_dn", [L, DFF, D], BF16)
    kT_s = dram_tmp("kT_s", [L, D, S], BF16)
    v_s = dram_tmp("v_s", [L, S, D], BF16)
    xs = dram_tmp("x_s", [max(L - 1, 1), D, S], F32)

    def sb(name, shape, dt):
        return stack.enter_context(nc.sbuf_tensor(name, list(shape), dt))

    def pst(name):
        return stack.enter_context(nc.psum_tensor(name, [128, 512], F32))

    NW = 3
    wring = [sb("wr%d" % i, [128, 8192], BF16) for i in range(NW)]
    pp = sb("pp_sb", [128, L * NPP], F32)
    der = sb("der", [128, L * 40], F32)
    ones_bf = sb("ones_bf", [128, 128], BF16)
    ones_ln = sb("ones_ln", [128, 128], BF16)
    ones_sub = sb("ones_sub", [128, 128], BF16)
    ones_f = sb("ones_f", [128, 128], F32)
    perm = sb("perm_sb", [128, 128], BF16)
    masks = sb("mask_sb", [128, TB + 384], BF16)
    cos_t = sb("cos_t", [128, TB], F32)
    sin_t = sb("sin_t", [128, TB], F32)
    xf = [sb("xf%d" % c, [128, TB], F32) for c in range(KC)]
    xb = [sb("xb%d" % c, [128, TB], BF16) for c in range(KC)]
    carry = sb("carry", [128, KC * 3], F32)
    hstate = sb("hstate", [128, KC], F32)
    NFT = 9
    ft = [sb("ft%d" % i, [128, 3 + TB], F32) for i in range(NFT)]
    xcp = [sb("xc%d" % i, [128, TB], F32) for i in range(4)]
    NBT = 8
    bt = [sb("bt%d" % i, [128, TB], BF16) for i in range(NBT)]
    hr = [sb("hr%d" % c, [128, TB], BF16) for c in range(KC)]
    qb = [sb("qb%d" % c, [128, TB], BF16) for c in range(KC)]
    ob = [sb("ob%d" % c, [128, TB], BF16) for c in range(KC)]
    kst = [sb("kst%d" % i, [128, 4 * TB], BF16) for i in range(1)]
    NKV = 2
    Kt = [sb("Kt%d" % i, [128, 2 * TB], BF16) for i in range(NKV)]
    Vt = [sb("Vt%d" % i, [128, 4 * 256], BF16) for i in range(NKV)]
    Et = [sb("Et%d" % i, [128, TB], BF16) for i in range(4)]
    act = hr + ob + qb[:FC - 2 * KC]
    act_ids = [("hr", c) for c in range(KC)] + [("ob", c) for c in range(KC)] + [("qb", c) for c in range(FC - 2 * KC)]
    ps = [pst("ps%d" % i) for i in range(8)]

    ctr = {"ft": 0, "bt": 0, "ps": 0, "w": 0, "kv": 0, "kst": 0}

    def nxt(kind, n):
        i = ctr[kind]
        ctr[kind] = (i + 1) % n
        return i

    gens = {"n": 0}

    def FT():
        i = nxt("ft", NFT)
        gens["n"] += 1
        sc.gen[("ft", i)] = gens["n"]
        return ft[i], ("ft", i, gens["n"])

    def BT():
        i = nxt("bt", NBT)
        gens["n"] += 1
        sc.gen[("bt", i)] = gens["n"]
        return bt[i], ("bt", i, gens["n"])

    def PS():
        i = nxt("ps", 8)
        return ps[i], ("ps", i)

    def dma(eng, out, in_, reads, writes, key):
        sc.add(eng, lambda e: e.dma_start(out=out, in_=in_), reads=reads, writes=writes, dma_key=key)

    def mm(out, lhsT, rhs, start, stop, reads, writes):
        sc.add("pe", lambda e: e.matmul(out, lhsT, rhs, start=start, stop=stop), reads=reads, writes=writes)

    def actf(out, in_, func, reads, writes, bias=None, scale=None):
        kw = {}
        if bias is not None:
            kw["bias"] = bias
        if scale is not None:
            kw["scale"] = scale
        sc.add("act", lambda e: e.activation(out, in_, func, **kw), reads=reads, writes=writes)

    def tt(eng, out, in0, in1, op, reads, writes):
        sc.add(eng, lambda e: e.tensor_tensor(out, in0, in1, op), reads=reads, writes=writes)

    def tsc(eng, out, in0, s1, s2, op0, op1, reads, writes):
        if op1 is None:
            sc.add(eng, lambda e: e.tensor_scalar(out, in0, s1, None, op0), reads=reads, writes=writes)
        else:
            sc.add(eng, lambda e: e.tensor_scalar(out, in0, s1, s2, op0, op1), reads=reads, writes=writes)

    def stt(eng, out, in0, scalar, in1, op0, op1, reads, writes):
        sc.add(eng, lambda e: e.scalar_tensor_tensor(out, in0, scalar, in1, op0, op1), reads=reads, writes=writes)

    def cp(eng, out, in_, reads, writes):
        if eng == "act":
            sc.add(eng, lambda e: e.activation(out, in_, AF.Copy), reads=reads, writes=writes)
        else:
            sc.add(eng, lambda e: e.tensor_copy(out, in_), reads=reads, writes=writes)

    def recip(out, in_, reads, writes):
        sc.add("dve", lambda e: e.reciprocal(out, in_), reads=reads, writes=writes)

    def memset(eng, ap, val, writes):
        sc.add(eng, lambda e: e.memset(ap, val), writes=writes)

    memset("dve", ones_bf[:], 1.0, ["ones_bf"])
    memset("dve", ones_ln[:], 1.0 / D, ["ones_ln"])
    memset("dve", ones_sub[:], 1.0 / 256.0, ["ones_sub"])
    memset("dve", ones_f[:], 1.0, ["ones_f"])
    dma("sp", pp[:], pp_in[:, :], [], ["pp"], "c0")
    dma("sp", perm[:], perm_in[:, :], [], ["perm"], "c1")
    dma("sp", masks[:], mask_in[:, :], [], ["masks"], "c2")

    pending_cv = []
    cvn = [0, 0]

    def pump(n):
        for _ in range(n):
            if pending_cv:
                dst, src, wid = pending_cv.pop(0)
                cvn[0] += 1
                dma("pool", dst, src, [], [wid], ("cv", cvn[0] % 8))

    def convert_layer(l, defer=False):
        RB = 128

        def cv(dst, src, wid):
            pending_cv.append((dst, src, wid))
            if not defer:
                pump(1)

        secs = [(6144, 10240), (2048, 6144), (0, 2048), (10240, 14336)]
        for si, (c0, c1) in enumerate(secs):
            for r0 in range(0, D, RB):
                cv(wb_in[l, r0:r0 + RB, c0:c1], w_in[l, r0:r0 + RB, c0:c1], ("cw_in", l, si, r0))
            if si == 0:
                for grp in range(4):
                    for gi in range(2):
                        for n in range(2):
                            for k in range(2):
                                o = ((gi * 2 + n) * 2 + k) * 256
                                cv(wb_rg[l, grp, :, o:o + 256], w_rg[l, gi, grp * 2 + n, k * 128:(k + 1) * 128, :],
                                   ("cw_rg", l, grp, gi, n, k))
        for j in range(2):
            for r0 in range(0, D, RB):
                cv(wb_br[l, j, r0:r0 + RB, :], w_br[l, j, r0:r0 + RB, :], ("cw_br", l, j, r0))
        for r0 in range(0, D, RB):
            cv(wb_out[l, r0:r0 + RB, :], w_out[l, r0:r0 + RB, :], ("cw_out", l, r0))
        for r0 in range(0, D, RB):
            cv(wb_gu[l, r0:r0 + RB, :], w_gu[l, r0:r0 + RB, :], ("cw_gu", l, r0))
        for r0 in range(0, DFF, RB):
            cv(wb_dn[l, r0:r0 + RB, :], w_dn[l, r0:r0 + RB, :], ("cw_dn", l, r0))

    def in_sec(c0):
        return 0 if 6144 <= c0 < 10240 else (1 if 2048 <= c0 < 6144 else (2 if c0 < 2048 else 3))

    def wtile(src2d, kcn, ncols, dep, r0=0):
        cvn[1] += 1
        if cvn[1] % 3 == 0:
            pump(1)
        i = nxt("w", NW)
        view = wring[i][:, 0:kcn * ncols].rearrange("p (k n) -> p k n", n=ncols)
        deps = [dep + (r0 + k * 128,) for k in range(kcn)]
        dma("sp", view, src2d.rearrange("(k p) n -> p k n", p=128), deps, [("w", i)], ("w", i))
        return view, ("w", i)

    def derive(l):
        lam_init = 0.8 - 0.6 * math.exp(-0.3 * (l + layer0))
        P0 = l * NPP
        d0 = l * 40
        actf(der[:, d0:d0 + 16], pp[:, P0 + O_LAM:P0 + O_LAM + 16], AF.Exp, ["pp"], [("der", l)], scale=-1.0)
        actf(der[:, d0:d0 + 16], der[:, d0:d0 + 16], AF.Ln, [("der", l)], [("der", l)], bias=1.0)
        tsc("dve", der[:, d0 + 16:d0 + 32], der[:, d0:d0 + 16], -16.0, None, ALU.mult, None, [("der", l)], [("der", l)])
        tsc("dve", der[:, d0:d0 + 16], der[:, d0:d0 + 16], -8.0, None, ALU.mult, None, [("der", l)], [("der", l)])
        tsc("dve", der[:, d0 + 32:d0 + 34], pp[:, P0 + O_SUB:P0 + O_SUB + 2], 1.0 - lam_init, None, ALU.mult, None,
            ["pp", ("der", l)], [("der", l)])
        tt("dve", der[:, d0 + 36:d0 + 37], pp[:, P0 + O_DL:P0 + O_DL + 1], pp[:, P0 + O_DL + 1:P0 + O_DL + 2], ALU.mult,
           ["pp", ("der", l)], [("der", l)])
        tt("dve", der[:, d0 + 37:d0 + 38], pp[:, P0 + O_DL + 2:P0 + O_DL + 3], pp[:, P0 + O_DL + 3:P0 + O_DL + 4], ALU.mult,
           ["pp", ("der", l)], [("der", l)])
        p, pid = PS()
        mm(p[:, 0:2], ones_f[:], der[:, d0 + 36:d0 + 38], True, True, ["ones_f", ("der", l)], [pid])
        actf(der[:, d0 + 36:d0 + 38], p[:, 0:2], AF.Exp, [pid], [("der", l)])
        tt("dve", der[:, d0 + 38:d0 + 39], der[:, d0 + 37:d0 + 38], der[:, d0 + 36:d0 + 37], ALU.subtract,
           [("der", l)], [("der", l)])
        tsc("dve", der[:, d0 + 34:d0 + 35], der[:, d0 + 38:d0 + 39], -lam_init, None, ALU.add, None,
            [("der", l)], [("der", l)])

    def ppc(l, off, c):
        return pp[:, l * NPP + off + c:l * NPP + off + c + 1]

    STOP = int(os.environ.get("KSTOP", "99"))

    def block(l, tb):
        try:
            block_body(l, tb)
        except StopIteration:
            pass
        dst = outT if l == L - 1 else xs[l]
        for c in range(KC):
            dma("pool", dst[c * 128:(c + 1) * 128, tb * TB:(tb + 1) * TB], xf[c][:], [("xf", c)], [("xs", l, tb, c)], ("xst", c))

    def phase(k):
        sc.tag = "ph%d" % k
        if STOP < k:
            raise StopIteration

    def x_src(l, tb, c):
        src = xT_in if l == 0 else xs[l - 1]
        return src[c * 128:(c + 1) * 128, tb * TB:(tb + 1) * TB], ("xs", l - 1, tb, c)

    def load_xf(l, tb):
        for c in range(KC):
            s_ap, s_id = x_src(l, tb, c)
            dma("sp", xf[c][:], s_ap, [s_id], [("xf", c)], ("xl", c))

    def prefetch_xb(l, tb):
        for c in range(KC):
            s_ap, s_id = x_src(l, tb, c)
            t_, tid = FT()
            dma("act", t_[:, 0:TB], s_ap, [s_id], [tid], ("xp", c % 8))
            cp("dve", xb[c][:], t_[:, 0:TB], [tid], [("xb", c)])

    def block_body(l, tb):
        t0 = tb * TB
        phase(1)
        last_layer = (l == L - 1)
        DER = ("der", l)
        d0 = l * 40
        first = (l == 0 and tb == 0) or STOP < 8
        if first:
            load_xf(l, tb)
        dma("sp", cos_t[:], cos_in[:, t0:t0 + TB], [], ["cos"], "cos")
        dma("sp", sin_t[:], sin_in[:, t0:t0 + TB], [], ["sin"], "sin")
        if first:
            for c in range(KC):
                cp("act" if c % 2 else "dve", xb[c][:], xf[c][:], [("xf", c)], [("xb", c)])

        xb_ids = [("xb", c) for c in range(KC)]

        def proj_fm(wv, wid, col, rhs, rhs_ids, nk):
            p, pid = PS()
            for kc in range(nk):
                mm(p[:], wv[:, kc, col * 128:(col + 1) * 128], rhs[kc][:], kc == 0, kc == nk - 1,
                   [wid, rhs_ids[kc]], [pid])
            return p, pid

        phase(2)
        for g in range(4):
            wv, wid = wtile(wb_in[l, :, 6144 + g * 512:6144 + (g + 1) * 512], KC, 512, ("cw_in", l, 0))
            xc_t = []
            for j in range(4):
                c = g * 4 + j
                p, pid = proj_fm(wv, wid, j, xb, xb_ids, KC)
                x_t, x_id = FT()
                cp("dve", x_t[:, 0:3], carry[:, c * 3:c * 3 + 3], [("carry", c)], [x_id])
                actf(x_t[:, 3:3 + TB], p[:], AF.Copy, [pid], [x_id])
                cp("dve", carry[:, c * 3:c * 3 + 3], x_t[:, TB:TB + 3], [x_id], [("carry", c)])
                c_t, c_id = xcp[j], ("xc", j)
                tsc("dve", c_t[:, 0:TB], x_t[:, 0:TB], ppc(l, O_CW, 0 * 16 + c), ppc(l, O_CB, c), ALU.mult, ALU.add,
                    [x_id, "pp"], [c_id])
                for w in range(1, 4):
                    stt("dve", c_t[:, 0:TB], x_t[:, w:w + TB], ppc(l, O_CW, w * 16 + c), c_t[:, 0:TB], ALU.mult, ALU.add,
                        [x_id, c_id, "pp"], [c_id])
                b_t, b_id = BT()
                cp("act", b_t[:], c_t[:, 0:TB], [c_id], [b_id])
                xc_t.append((c_t, c_id, b_t, b_id))
            wv, wid = wtile(wb_in[l, :, 8192 + g * 512:8192 + (g + 1) * 512], KC, 512, ("cw_in", l, 0))
            for j in range(4):
                c = g * 4 + j
                p, pid = proj_fm(wv, wid, j, xb, xb_ids, KC)
                g_t, g_id = FT()
                a_t, a_id = FT()
                cp("act", g_t[:, 0:TB], p[:], [pid], [g_id])
                actf(a_t[:, 0:TB], g_t[:, 0:TB], AF.Square, [g_id], [a_id])
                tsc("dve", a_t[:, 0:TB], a_t[:, 0:TB], 0.044715, 1.0, ALU.mult, ALU.add, [a_id], [a_id])
                tt("dve", a_t[:, 0:TB], a_t[:, 0:TB], g_t[:, 0:TB], ALU.mult, [a_id, g_id], [a_id])
                actf(a_t[:, 0:TB], a_t[:, 0:TB], AF.Sigmoid, [a_id], [a_id], scale=GELU_C)
                tt("dve", hr[c][:], a_t[:, 0:TB], g_t[:, 0:TB], ALU.mult, [a_id, g_id], [("hr", c)])
            wi = nxt("w", NW)
            wrg_sb = wring[wi]
            dma("sp", wrg_sb[:, 0:2048], wb_rg[l, g], [("cw_rg", l, g, gi, n, k) for gi in range(2) for n in range(2) for k in range(2)],
                [("w", wi)], ("w", wi))
            for j in range(4):
                c = g * 4 + j
                n, m = j // 2, c % 2
                pr, prid = PS()
                pi, piid = PS()
                for gi, (pp_, ppid) in enumerate(((pr, prid), (pi, piid))):
                    for kc in range(2):
                        base = ((gi * 2 + n) * 2 + kc) * 256 + m * 128
                        bsrc = xc_t[n * 2 + kc]
                        mm(pp_[:], wrg_sb[:, base:base + 128], bsrc[2][:], kc == 0, kc == 1,
                           [("w", wi), bsrc[3]], [ppid])
                c_t, c_id = xc_t[j][0], xc_t[j][1]
                r_t, r_id = FT()
                i_t, i_id = FT()
                actf(r_t[:, 0:TB], pr[:], AF.Sigmoid, [prid, "pp"], [r_id], bias=ppc(l, O_BRG, c))
                actf(i_t[:, 0:TB], pi[:], AF.Sigmoid, [piid, "pp"], [i_id], bias=ppc(l, O_BRG, 16 + c))
                a_t, a_id = FT()
                actf(a_t[:, 0:TB], r_t[:, 0:TB], AF.Exp, [r_id, DER], [a_id], scale=der[:, d0 + c:d0 + c + 1])
                actf(r_t[:, 0:TB], r_t[:, 0:TB], AF.Exp, [r_id, DER], [r_id], scale=der[:, d0 + 16 + c:d0 + 17 + c])
                actf(r_t[:, 0:TB], r_t[:, 0:TB], AF.Ln, [r_id], [r_id], scale=-1.0, bias=1.0)
                actf(r_t[:, 0:TB], r_t[:, 0:TB], AF.Exp, [r_id], [r_id], scale=0.5)
                tt("dve", i_t[:, 0:TB], i_t[:, 0:TB], c_t[:, 0:TB], ALU.mult, [i_id, c_id], [i_id])
                tt("dve", i_t[:, 0:TB], i_t[:, 0:TB], r_t[:, 0:TB], ALU.mult, [i_id, r_id], [i_id])
                sc.add("dve", lambda e, o=r_t[:, 0:TB], a=a_t[:, 0:TB], u=i_t[:, 0:TB], h0=hstate[:, c:c + 1]:
                       e.tensor_tensor_scan(o, a, u, h0, ALU.mult, ALU.add),
                       reads=[a_id, i_id, ("hs", c)], writes=[r_id])
                cp("dve", hstate[:, c:c + 1], r_t[:, TB - 1:TB], [r_id], [("hs", c)])
                tt("dve", hr[c][:], r_t[:, 0:TB], hr[c][:], ALU.mult, [r_id, ("hr", c)], [("hr", c)])

        rope_pend = []

        def rope_flush(keep=0):
            while len(rope_pend) > keep:
                p, pid, b_t, b_id, out_ap, out_id, after = rope_pend.pop(0)
                p2, p2id = PS()
                mm(p2[:], perm[:], b_t[:], True, True, ["perm", b_id], [p2id])
                f1, f1id = FT()
                f2, f2id = FT()
                tt("dve", f1[:, 0:TB], p[:], cos_t[:], ALU.mult, [pid, "cos"], [f1id])
                tt("dve", f2[:, 0:TB], p2[:], sin_t[:], ALU.mult, [p2id, "sin"], [f2id])
                tt("dve", out_ap, f1[:, 0:TB], f2[:, 0:TB], ALU.add, [f1id, f2id], [out_id])
                if after is not None:
                    after()

        def rope_epilogue(p, pid, out_ap, out_id, after=None):
            b_t, b_id = BT()
            cp("act", b_t[:], p[:], [pid], [b_id])
            rope_pend.append((p, pid, b_t, b_id, out_ap, out_id, after))

        phase(3)
        for g in range(4):
            wv, wid = wtile(wb_in[l, :, 2048 + g * 512:2048 + (g + 1) * 512], KC, 512, ("cw_in", l, 1))
            si = nxt("kst", 1)

            def k_store(g=g, si=si):
                dma("pool", kT_s[l, g * 512:(g + 1) * 512, t0:t0 + TB].rearrange("(c p) t -> p c t", p=128),
                    kst[si][:, :].rearrange("p (c t) -> p c t", t=TB), [("kst", si)], [("kT", l, g, tb)], ("kst", si))

            for j in range(4):
                p, pid = proj_fm(wv, wid, j, xb, xb_ids, KC)
                rope_flush(keep=2)
                rope_epilogue(p, pid, kst[si][:, j * TB:(j + 1) * TB], ("kst", si), after=(k_store if j == 3 else None))
        rope_flush()
        for g in range(4):
            wv, wid = wtile(wb_in[l, :, 4096 + g * 512:4096 + (g + 1) * 512], KC, 512, ("cw_in", l, 1))
            si = nxt("kst", 1)
            for ts in range(4):
                p, pid = PS()
                for kc in range(KC):
                    mm(p[:], xb[kc][:, ts * 128:(ts + 1) * 128], wv[:, kc, :], kc == 0, kc == KC - 1,
                       [wid, ("xb", kc)], [pid])
                cp("act" if ts % 2 else "dve", kst[si][:, ts * TB:(ts + 1) * TB], p[:], [pid], [("kst", si)])
            dma("pool", v_s[l, t0:t0 + TB, g * 512:(g + 1) * 512].rearrange("(s p) e -> p s e", p=128),
                kst[si][:, :].rearrange("p (s e) -> p s e", e=512), [("kst", si)], [("v", l, g, tb)], ("kst", si))

        if not first:
            load_xf(l, tb)
        phase(4)
        for g in range(4):
            wv, wid = wtile(wb_in[l, :, g * 512:(g + 1) * 512], KC, 512, ("cw_in", l, 2))
            for j in range(4):
                c = g * 4 + j
                p, pid = proj_fm(wv, wid, j, xb, xb_ids, KC)
                rope_flush(keep=2)
                rope_epilogue(p, pid, qb[c][:], ("qb", c))
        rope_flush()

        phase(5)
        S1, S2 = (ps[0], ("ps", 0)), (ps[1], ("ps", 1))
        OA = [[(ps[2], ("ps", 2)), (ps[3], ("ps", 3))], [(ps[4], ("ps", 4)), (ps[5], ("ps", 5))]]
        LS = [(ps[6], ("ps", 6)), (ps[7], ("ps", 7))]
        scale = HD ** -0.5
        nchunks = 4 * (tb + 1)
        kv_tiles = {}

        def load_kv(h, kb):
            i = nxt("kv", NKV)
            dma("pool", Kt[i][:, :].rearrange("p (c t) -> p c t", t=TB),
                kT_s[l, h * 256:(h + 1) * 256, kb * TB:(kb + 1) * TB].rearrange("(c p) t -> p c t", p=128),
                [("kT", l, h // 2, kb)], [("Kt", i)], ("Kt", i))
            dma("pool", Vt[i][:, :].rearrange("p (s e) -> p s e", e=256),
                v_s[l, kb * TB:(kb + 1) * TB, h * 256:(h + 1) * 256].rearrange("(s p) e -> p s e", p=128),
                [("v", l, h // 2, kb)], [("Vt", i)], ("Vt", i))
            kv_tiles[(h, kb)] = i

        def scores(h, j):
            kb, ks = j // 4, j % 4
            if (h, kb) not in kv_tiles:
                load_kv(h, kb)
            i = kv_tiles[(h, kb)]
            q0 = 128 * ks if kb == tb else 0
            for cc, (sp_, spid) in enumerate((S1, S2)):
                mm(sp_[:, q0:TB], Kt[i][:, cc * TB + ks * 128:cc * TB + (ks + 1) * 128], qb[2 * h + cc][:, q0:TB], True, True,
                   [("Kt", i), ("qb", 2 * h + cc)], [spid])
                e_i = (j % 2) * 2 + cc
                actf(Et[e_i][:, q0:TB], sp_[:, q0:TB], AF.Exp, [spid], [("Et", e_i)], scale=scale)
                if kb == tb:
                    tt("dve", Et[e_i][:, q0:TB], Et[e_i][:, q0:TB], masks[:, 384 - 128 * ks + q0:384 - 128 * ks + TB], ALU.mult,
                       [("Et", e_i), "masks"], [("Et", e_i)])

        def av(h, j):
            kb, ks = j // 4, j % 4
            i = kv_tiles[(h, kb)]
            q0 = 128 * ks if kb == tb else 0
            st, sp = (j == 0), (j == nchunks - 1)
            for cc in range(2):
                e_i = (j % 2) * 2 + cc
                for ec in range(2):
                    o_, oid = OA[cc][ec]
                    mm(o_[:, q0:TB], Vt[i][:, ks * 256 + ec * 128:ks * 256 + (ec + 1) * 128], Et[e_i][:, q0:TB], st, sp,
                       [("Vt", i), ("Et", e_i)], [oid])
                l_, lid = LS[cc]
                mm(l_[:, q0:TB], ones_bf[:], Et[e_i][:, q0:TB], st, sp, ["ones_bf", ("Et", e_i)], [lid])

        def epilogue_a(h):
            cpy = [None] * 4
            for ec in range(2):
                c_t, c_id = FT()
                cp("act", c_t[:, 0:TB], OA[0][ec][0][:], [OA[0][ec][1]], [c_id])
                cpy[ec] = (c_t, c_id)
            for ec in range(2):
                c_t, c_id = FT()
                cp("dve", c_t[:, 0:TB], OA[1][ec][0][:], [OA[1][ec][1]], [c_id])
                cpy[2 + ec] = (c_t, c_id)
            rl = []
            for cc in range(2):
                r_t, r_id = FT()
                recip(r_t[:, 0:TB], LS[cc][0][:], [LS[cc][1]], [r_id])
                rl.append((r_t, r_id))
            od = []
            for ec in range(2):
                a_t, a_id = cpy[ec]
                b_t, b_id = cpy[2 + ec]
                tt("dve", a_t[:, 0:TB], a_t[:, 0:TB], rl[0][0][:, 0:TB], ALU.mult, [a_id, rl[0][1]], [a_id])
                tt("dve", b_t[:, 0:TB], b_t[:, 0:TB], rl[1][0][:, 0:TB], ALU.mult, [b_id, rl[1][1]], [b_id])
                stt("dve", a_t[:, 0:TB], b_t[:, 0:TB], der[:, d0 + 34:d0 + 35], a_t[:, 0:TB], ALU.mult, ALU.add,
                    [a_id, b_id, DER], [a_id])
                q_t, q_id = BT()
                tt("dve", q_t[:], a_t[:, 0:TB], a_t[:, 0:TB], ALU.mult, [a_id], [q_id])
                od.append((a_t, a_id, q_t, q_id))
            return od

        def epilogue_b(h, od):
            p, pid = S1
            for ec in range(2):
                mm(p[:], ones_sub[:], od[ec][2][:], ec == 0, ec == 1, ["ones_sub", od[ec][3]], [pid])
            r_t, r_id = FT()
            actf(r_t[:, 0:TB], p[:], AF.Ln, [pid], [r_id], bias=EPS)
            actf(r_t[:, 0:TB], r_t[:, 0:TB], AF.Exp, [r_id], [r_id], scale=-0.5)
            for ec in range(2):
                stt("dve", ob[2 * h + ec][:], od[ec][0][:, 0:TB], der[:, d0 + 32 + ec:d0 + 33 + ec], r_t[:, 0:TB],
                    ALU.mult, ALU.mult, [od[ec][1], r_id, DER], [("ob", 2 * h + ec)])

        pend = None
        jdef = min(3, nchunks - 1)
        scores(0, 0)
        for h in range(NH):
            for j in range(nchunks):
                if j + 1 < nchunks:
                    scores(h, j + 1)
                av(h, j)
                if j == jdef and pend is not None:
                    epilogue_b(*pend)
                    pend = None
            if h + 1 < NH:
                scores(h + 1, 0)
            od = epilogue_a(h)
            pend = (h, od)
        epilogue_b(*pend)

        phase(6)
        mg = qb
        ob_ids = [("ob", c) for c in range(KC)]
        hr_ids = [("hr", c) for c in range(KC)]
        for g in range(4):
            gts = []
            for which in range(2):
                wv, wid = wtile(wb_in[l, :, 10240 + which * 2048 + g * 512:10240 + which * 2048 + (g + 1) * 512],
                                KC, 512, ("cw_in", l, 3))
                for j in range(4):
                    c = g * 4 + j
                    p, pid = proj_fm(wv, wid, j, xb, xb_ids, KC)
                    g_t, g_id = FT()
                    actf(g_t[:, 0:TB], p[:], AF.Sigmoid, [pid, "pp"], [g_id], bias=ppc(l, O_BM, which * 16 + c))
                    gts.append((g_t, g_id))
            wv, wid = wtile(wb_br[l, 0, :, g * 512:(g + 1) * 512], KC, 512, ("cw_br", l, 0))
            for j in range(4):
                p, pid = proj_fm(wv, wid, j, ob, ob_ids, KC)
                g_t, g_id = gts[j]
                tt("dve", g_t[:, 0:TB], g_t[:, 0:TB], p[:], ALU.mult, [g_id, pid], [g_id])
            wv, wid = wtile(wb_br[l, 1, :, g * 512:(g + 1) * 512], KC, 512, ("cw_br", l, 1))
            for j in range(4):
                c = g * 4 + j
                p, pid = proj_fm(wv, wid, j, hr, hr_ids, KC)
                g_t, g_id = gts[4 + j]
                tt("dve", g_t[:, 0:TB], g_t[:, 0:TB], p[:], ALU.mult, [g_id, pid], [g_id])
                tt("dve", mg[c][:], g_t[:, 0:TB], gts[j][0][:, 0:TB], ALU.add, [g_id, gts[j][1]], [("qb", c)])

        def layer_norm(j, make_bf):
            pm, pmid = PS()
            pq, pqid = PS()
            for c in range(KC):
                b1, b1id = BT()
                b2, b2id = BT()
                cp("act", b1[:], xf[c][:], [("xf", c)], [b1id])
                actf(b2[:], xf[c][:], AF.Square, [("xf", c)], [b2id])
                mm(pm[:], ones_ln[:], b1[:], c == 0, c == KC - 1, ["ones_ln", b1id], [pmid])
                mm(pq[:], ones_ln[:], b2[:], c == 0, c == KC - 1, ["ones_ln", b2id], [pqid])
            mu, muid = FT()
            rs, rsid = FT()
            cp("dve", mu[:, 0:TB], pm[:], [pmid], [muid])
            tt("dve", rs[:, 0:TB], mu[:, 0:TB], mu[:, 0:TB], ALU.mult, [muid], [rsid])
            tt("dve", rs[:, 0:TB], pq[:], rs[:, 0:TB], ALU.subtract, [pqid, rsid], [rsid])
            actf(rs[:, 0:TB], rs[:, 0:TB], AF.Ln, [rsid], [rsid], bias=EPS)
            actf(rs[:, 0:TB], rs[:, 0:TB], AF.Exp, [rsid], [rsid], scale=-0.5)
            for c in range(KC):
                tt("dve", xf[c][:], xf[c][:], mu[:, 0:TB], ALU.subtract, [("xf", c), muid], [("xf", c)])
                tt("dve", xf[c][:], xf[c][:], rs[:, 0:TB], ALU.mult, [("xf", c), rsid], [("xf", c)])
                if make_bf:
                    actf(xb[c][:], xf[c][:], AF.Identity, [("xf", c), "pp"], [("xb", c)],
                         bias=ppc(l, O_LNB, j * 16 + c), scale=ppc(l, O_LNG, j * 16 + c))
                actf(xf[c][:], xf[c][:], AF.Identity, [("xf", c), "pp"], [("xf", c)],
                     bias=ppc(l, O_LNB, j * 16 + c), scale=ppc(l, O_LNG, j * 16 + c))


        phase(7)
        mg_ids = [("qb", c) for c in range(KC)]
        for g in range(4):
            wv, wid = wtile(wb_out[l, :, g * 512:(g + 1) * 512], KC, 512, ("cw_out", l))
            for j in range(4):
                c = g * 4 + j
                p, pid = proj_fm(wv, wid, j, mg, mg_ids, KC)
                stt("dve", xf[c][:], xf[c][:], ALPHA, p[:], ALU.mult, ALU.add, [("xf", c), pid], [("xf", c)])
        layer_norm(0, True)

        phase(8)
        for g in range(11):
            wv, wid = wtile(wb_gu[l, :, g * 512:(g + 1) * 512], KC, 512, ("cw_gu", l))
            sg = []
            for j in range(4):
                p, pid = proj_fm(wv, wid, j, xb, xb_ids, KC)
                s_t, s_id = FT()
                actf(s_t[:, 0:TB], p[:], AF.Silu, [pid], [s_id])
                sg.append((s_t, s_id))
            wv, wid = wtile(wb_gu[l, :, DFF + g * 512:DFF + (g + 1) * 512], KC, 512, ("cw_gu", l))
            for j in range(4):
                c = g * 4 + j
                p, pid = proj_fm(wv, wid, j, xb, xb_ids, KC)
                tt("dve", act[c][:], sg[j][0][:, 0:TB], p[:], ALU.mult, [sg[j][1], pid], [act_ids[c]])
        nb = (l, tb + 1) if tb + 1 < NTB else ((l + 1, 0) if l + 1 < L else None)
        for g in range(8):
            if g == 2 and nb is not None:
                prefetch_xb(*nb)
            pa = [PS(), PS()]
            for half in range(2):
                wv, wid = wtile(wb_dn[l, half * 2816:(half + 1) * 2816, g * 256:(g + 1) * 256], 22, 256, ("cw_dn", l), r0=half * 2816)
                for j in range(2):
                    for kc in range(22):
                        k = half * 22 + kc
                        mm(pa[j][0][:], wv[:, kc, j * 128:(j + 1) * 128], act[k][:], k == 0, k == FC - 1,
                           [wid, act_ids[k]], [pa[j][1]])
            for j in range(2):
                c = g * 2 + j
                stt("dve", xf[c][:], xf[c][:], ALPHA, pa[j][0][:], ALU.mult, ALU.add, [("xf", c), pa[j][1]], [("xf", c)])
        layer_norm(1, False)

    convert_layer(0)
    for l in range(L):
        pump(len(pending_cv))
        if l + 1 < L:
            convert_layer(l + 1, defer=True)
        derive(l)
        memset("dve", carry[:], 0.0, [("carry", c) for c in range(KC)])
        memset("dve", hstate[:], 0.0, [("hs", c) for c in range(KC)])
        for tb in range(NTB):
            block(l, tb)
    sc.add("pool", lambda e: e.engine_nop(), reads=[("xs", L - 1, tb, c) for tb in range(NTB) for c in range(KC)],
           writes=["final"])
    sc.emit(nc, stack)
    stack.close()
    nc._sched = sc
    return nc


def _pack_pp(inp, layers):
    cols = []
    for l in layers:
        def fm(a):
            a = np.asarray(a, np.float32).reshape(-1, KC, 128)
            return np.ascontiguousarray(a.transpose(2, 0, 1).reshape(128, -1))
        blk = [fm(inp["b_merge"][l]), fm(inp["conv_w"][l]), fm(inp["conv_b"][l]), fm(inp["b_rg"][l]),
               fm(inp["lru_lambda"][l]), fm(inp["ln_g"][l]), fm(inp["ln_b"][l]),
               np.ascontiguousarray(np.asarray(inp["subln_g"][l], np.float32).reshape(2, 128).T),
               np.ascontiguousarray(np.asarray(inp["diff_lambda"][l], np.float32).T)]
        blk = np.concatenate(blk, axis=1)
        assert blk.shape == (128, NPP), blk.shape
        cols.append(blk)
    return np.ascontiguousarray(np.concatenate(cols, axis=1))


def _consts(S):
    import ml_dtypes
    half = HD // 2
    inv = (10000.0 ** (-np.arange(half, dtype=np.float32) * 2.0 / HD)).astype(np.float32)
    ang = np.arange(S, dtype=np.float32)[:, None] * inv[None, :]
    ang = np.concatenate([ang, ang], axis=-1)
    cosT = np.ascontiguousarray(np.cos(ang).T.astype(np.float32))
    sgn = np.where(np.arange(HD) < half, -1.0, 1.0).astype(np.float32)
    sinS = np.ascontiguousarray((np.sin(ang) * sgn[None, :]).T.astype(np.float32))
    perm = np.zeros((128, 128), np.float32)
    for d2 in range(128):
        perm[(d2 + 64) % 128, d2] = 1.0
    k = np.arange(128)[:, None]
    xx = np.arange(TB + 384)[None, :]
    masks = ((xx - 384) >= k).astype(np.float32)
    return cosT, sinS, perm.astype(ml_dtypes.bfloat16), masks.astype(ml_dtypes.bfloat16)


_PROG_CACHE = {}


def _run(xT_list, inp, layers, S, layer0):
    key = (len(layers), S, layer0)
    if key not in _PROG_CACHE:
        _PROG_CACHE[key] = build_program(len(layers), S, layer0=layer0)
    nc = _PROG_CACHE[key]
    cosT, sinS, perm, masks = _consts(S)
    sl = slice(layers[0], layers[-1] + 1)
    shared = {
        "w_in": np.ascontiguousarray(inp["w_in"][sl]), "w_rg": np.ascontiguousarray(inp["w_rg"][sl]),
        "w_branch": np.ascontiguousarray(inp["w_branch"][sl]), "w_out": np.ascontiguousarray(inp["w_out"][sl]),
        "w_gate_up": np.ascontiguousarray(inp["w_gate_up"][sl]), "w_down": np.ascontiguousarray(inp["w_down"][sl]),
        "pp": _pack_pp(inp, layers), "cosT": cosT, "sinS": sinS, "perm": perm, "masks": masks,
    }
    in_maps = [dict(shared, xT=xT) for xT in xT_list]
    res = run_bass_kernel_spmd(nc, in_maps, core_ids=list(range(len(xT_list))))
    return [np.asarray(r["outT"]) for r in res.results]


def kernel(**inputs):
    inp = {k: np.asarray(v) for k, v in inputs.items()}
    x = inp["x"].astype(np.float32, copy=False)
    B, S, _ = x.shape
    xT = [np.ascontiguousarray(x[b].T) for b in range(B)]
    outT = _run(xT, inp, list(range(DEPTH)), S, 0)
    return np.stack([o.T for o in outT], axis=0).astype(np.float32)
```

```python
import math
import os
from contextlib import ExitStack

import numpy as np
import concourse.bass as bass
import concourse.mybir as mybir
from concourse.bass_utils import run_bass_kernel_spmd

F32 = mybir.dt.float32
BF16 = mybir.dt.bfloat16
AF = mybir.ActivationFunctionType
ALU = mybir.AluOpType

D = 2048
KC = 16
HD = 128
NH = 8
DFF = 5632
FC = 44
N_IN = 14336
TB = 512
EPS = 1e-5
DEPTH = 4
ALPHA = (2.0 * DEPTH) ** 0.25
NPP = 230
O_BM, O_CW, O_CB, O_BRG, O_LAM, O_LNG, O_LNB, O_SUB, O_DL = 0, 32, 96, 112, 144, 160, 192, 224, 226
GELU_C = 2.0 * math.sqrt(2.0 / math.pi)


class _Op:
    __slots__ = ("eng", "fn", "deps", "dom", "sig", "seq", "dma", "tag")

    def __init__(self, eng, fn, deps, dom, dma):
        self.eng, self.fn, self.deps, self.dom, self.dma = eng, fn, deps, dom, dma
        self.sig = False
        self.seq = 0


class Sched:
    def __init__(self):
        self.ops = []
        self.lastw = {}
        self.readers = {}
        self.dma_last = {}
        self.tag = ""
        self.gen = {}

    def _norm(self, bufs):
        out = []
        for b in bufs:
            if isinstance(b, tuple) and len(b) == 3 and b[0] in ("ft", "bt"):
                assert self.gen[b[:2]] == b[2], ("stale temp tile", b, self.gen[b[:2]], self.tag)
                b = b[:2]
            out.append(b)
        return out

    def add(self, eng, fn, reads=(), writes=(), dma_key=None):
        idx = len(self.ops)
        deps = set()
        reads = self._norm(reads)
        writes = self._norm(writes)
        ex = [b for b in reads if isinstance(b, tuple) and b[0] == "ps"]
        if ex:
            reads = [b for b in reads if b not in ex]
            writes = list(writes) + ex
        for b in reads:
            w = self.lastw.get(b)
            if w is not None:
                deps.add(w)
        for b in writes:
            w = self.lastw.get(b)
            if w is not None:
                deps.add(w)
            rd = self.readers.get(b)
            if rd:
                deps.update(rd.values())
        dom = ("dma", dma_key) if dma_key is not None else eng
        if dma_key is not None:
            p = self.dma_last.get(dma_key)
            if p is not None:
                deps.add(p)
            self.dma_last[dma_key] = idx
        for b in reads:
            self.readers.setdefault(b, {})[dom] = idx
        for b in writes:
            self.lastw[b] = idx
            self.readers[b] = {}
        deps.discard(idx)
        if eng == "pe":
            deps = {d for d in deps if self.ops[d].eng != "pe"}
        op = _Op(eng, fn, deps, dom, dma_key is not None)
        op.tag = self.tag
        self.ops.append(op)
        return idx

    def emit(self, nc, stack):
        ops = self.ops
        for op in ops:
            if op.dma:
                op.sig = True
            for d in op.deps:
                ops[d].sig = True
        cnt = {}
        for op in ops:
            if op.sig:
                cnt[op.dom] = cnt.get(op.dom, 0) + 1
                op.seq = cnt[op.dom]
        sems = {}
        for i, dom in enumerate(cnt):
            sems[dom] = stack.enter_context(nc.semaphore("s%d" % i))
        per_eng = {}
        for i, op in enumerate(ops):
            per_eng.setdefault(op.eng, []).append(i)
        block = stack.enter_context(nc.Block())

        def run(eng_name, e):
            known = {}
            for i in per_eng.get(eng_name, []):
                op = ops[i]
                need = {}
                for d in op.deps:
                    o = ops[d]
                    if o.seq > need.get(o.dom, 0):
                        need[o.dom] = o.seq
                for dom, seq in need.items():
                    if known.get(dom, 0) >= seq:
                        continue
                    known[dom] = seq
                    unit = 16 if dom[0] == "dma" else 1
                    e.wait_ge(sems[dom], seq * unit)
                ins = op.fn(e)
                if op.sig:
                    ins.then_inc(sems[op.dom], 16 if op.dma else 1)

        @block.tensor
        def _(e):
            run("pe", e)

        @block.scalar
        def _(e):
            run("act", e)

        @block.vector
        def _(e):
            run("dve", e)

        @block.gpsimd
        def _(e):
            run("pool", e)

        @block.sync
        def _(e):
            run("sp", e)


def build_program(n_layers, S, layer0=0, dbg=False):
    NTB = S // TB
    L = n_layers
    nc = bass.Bass("TRN2", target_bir_lowering=False)
    sc = Sched()
    stack = ExitStack()

    def dram_in(name, shape, dt=F32):
        return nc.dram_tensor(name, list(shape), dt, kind="ExternalInput").ap()

    def dram_tmp(name, shape, dt):
        return nc.dram_tensor(name, list(shape), dt, kind="Internal").ap()

    xT_in = dram_in("xT", [D, S])
    w_in = dram_in("w_in", [L, D, N_IN])
    w_rg = dram_in("w_rg", [L, 2, 8, 256, 256])
    w_br = dram_in("w_branch", [L, 2, D, D])
    w_out = dram_in("w_out", [L, D, D])
    w_gu = dram_in("w_gate_up", [L, D, 2 * DFF])
    w_dn = dram_in("w_down", [L, DFF, D])
    pp_in = dram_in("pp", [128, L * NPP])
    cos_in = dram_in("cosT", [128, S])
    sin_in = dram_in("sinS", [128, S])
    perm_in = dram_in("perm", [128, 128], BF16)
    mask_in = dram_in("masks", [128, TB + 384], BF16)
    outT = nc.dram_tensor("outT", [D, S], F32, kind="ExternalOutput").ap()

    wb_in = dram_tmp("wb_in", [L, D, N_IN], BF16)
    wb_rg = dram_tmp("wb_rg", [L, 4, 128, 2048], BF16)
    wb_br = dram_tmp("wb_br", [L, 2, D, D], BF16)
    wb_out = dram_tmp("wb_out", [L, D, D], BF16)
    wb_gu = dram_tmp("wb_gu", [L, D, 2 * DFF], BF16)
    wb_dn = dram_tmp("wb_dn", [L, DFF, D], BF16)
    kT_s = dram_tmp("kT_s", [L, D, S], BF16)
    v_s = dram_tmp("v_s", [L, S, D], BF16)
    xs = dram_tmp("x_s", [max(L - 1, 1), D, S], F32)

    def sb(name, shape, dt):
        return stack.enter_context(nc.sbuf_tensor(name, list(shape), dt))

    def pst(name):
        return stack.enter_context(nc.psum_tensor(name, [128, 512], F32))

    NW = 3
    wring = [sb("wr%d" % i, [128, 8192], BF16) for i in range(NW)]
    pp = sb("pp_sb", [128, L * NPP], F32)
    der = sb("der", [128, L * 40], F32)
    ones_bf = sb("ones_bf", [128, 128], BF16)
    ones_ln = sb("ones_ln", [128, 128], BF16)
    ones_sub = sb("ones_sub", [128, 128], BF16)
    ones_f = sb("ones_f", [128, 128], F32)
    perm = sb("perm_sb", [128, 128], BF16)
    masks = sb("mask_sb", [128, TB + 384], BF16)
    cos_t = sb("cos_t", [128, TB], F32)
    sin_t = sb("sin_t", [128, TB], F32)
    xf = [sb("xf%d" % c, [128, TB], F32) for c in range(KC)]
    xb = [sb("xb%d" % c, [128, TB], BF16) for c in range(KC)]
    carry = sb("carry", [128, KC * 3], F32)
    hstate = sb("hstate", [128, KC], F32)
    NFT = 9
    ft = [sb("ft%d" % i, [128, 3 + TB], F32) for i in range(NFT)]
    xcp = [sb("xc%d" % i, [128, TB], F32) for i in range(4)]
    NBT = 8
    bt = [sb("bt%d" % i, [128, TB], BF16) for i in range(NBT)]
    hr = [sb("hr%d" % c, [128, TB], BF16) for c in range(KC)]
    qb = [sb("qb%d" % c, [128, TB], BF16) for c in range(KC)]
    ob = [sb("ob%d" % c, [128, TB], BF16) for c in range(KC)]
    kst = [sb("kst%d" % i, [128, 4 * TB], BF16) for i in range(1)]
    NKV = 2
    Kt = [sb("Kt%d" % i, [128, 2 * TB], BF16) for i in range(NKV)]
    Vt = [sb("Vt%d" % i, [128, 4 * 256], BF16) for i in range(NKV)]
    Et = [sb("Et%d" % i, [128, TB], BF16) for i in range(4)]
    act = hr + ob + qb[:FC - 2 * KC]
    act_ids = [("hr", c) for c in range(KC)] + [("ob", c) for c in range(KC)] + [("qb", c) for c in range(FC - 2 * KC)]
    ps = [pst("ps%d" % i) for i in range(8)]

    ctr = {"ft": 0, "bt": 0, "ps": 0, "w": 0, "kv": 0, "kst": 0}

    def nxt(kind, n):
        i = ctr[kind]
        ctr[kind] = (i + 1) % n
        return i

    gens = {"n": 0}

    def FT():
        i = nxt("ft", NFT)
        gens["n"] += 1
        sc.gen[("ft", i)] = gens["n"]
        return ft[i], ("ft", i, gens["n"])

    def BT():
        i = nxt("bt", NBT)
        gens["n"] += 1
        sc.gen[("bt", i)] = gens["n"]
        return bt[i], ("bt", i, gens["n"])

    def PS():
        i = nxt("ps", 8)
        return ps[i], ("ps", i)

    def dma(eng, out, in_, reads, writes, key):
        sc.add(eng, lambda e: e.dma_start(out=out, in_=in_), reads=reads, writes=writes, dma_key=key)

    def mm(out, lhsT, rhs, start, stop, reads, writes):
        sc.add("pe", lambda e: e.matmul(out, lhsT, rhs, start=start, stop=stop), reads=reads, writes=writes)

    def actf(out, in_, func, reads, writes, bias=None, scale=None):
        kw = {}
        if bias is not None:
            kw["bias"] = bias
        if scale is not None:
            kw["scale"] = scale
        sc.add("act", lambda e: e.activation(out, in_, func, **kw), reads=reads, writes=writes)

    def tt(eng, out, in0, in1, op, reads, writes):
        sc.add(eng, lambda e: e.tensor_tensor(out, in0, in1, op), reads=reads, writes=writes)

    def tsc(eng, out, in0, s1, s2, op0, op1, reads, writes):
        if op1 is None:
            sc.add(eng, lambda e: e.tensor_scalar(out, in0, s1, None, op0), reads=reads, writes=writes)
        else:
            sc.add(eng, lambda e: e.tensor_scalar(out, in0, s1, s2, op0, op1), reads=reads, writes=writes)

    def stt(eng, out, in0, scalar, in1, op0, op1, reads, writes):
        sc.add(eng, lambda e: e.scalar_tensor_tensor(out, in0, scalar, in1, op0, op1), reads=reads, writes=writes)

    def cp(eng, out, in_, reads, writes):
        if eng == "act":
            sc.add(eng, lambda e: e.activation(out, in_, AF.Copy), reads=reads, writes=writes)
        else:
            sc.add(eng, lambda e: e.tensor_copy(out, in_), reads=reads, writes=writes)

    def recip(out, in_, reads, writes):
        sc.add("dve", lambda e: e.reciprocal(out, in_), reads=reads, writes=writes)

    def memset(eng, ap, val, writes):
        sc.add(eng, lambda e: e.memset(ap, val), writes=writes)

    memset("dve", ones_bf[:], 1.0, ["ones_bf"])
    memset("dve", ones_ln[:], 1.0 / D, ["ones_ln"])
    memset("dve", ones_sub[:], 1.0 / 256.0, ["ones_sub"])
    memset("dve", ones_f[:], 1.0, ["ones_f"])
    dma("sp", pp[:], pp_in[:, :], [], ["pp"], "c0")
    dma("sp", perm[:], perm_in[:, :], [], ["perm"], "c1")
    dma("sp", masks[:], mask_in[:, :], [], ["masks"], "c2")

    pending_cv = []
    cvn = [0, 0]

    def pump(n):
        for _ in range(n):
            if pending_cv:
                dst, src, wid = pending_cv.pop(0)
                cvn[0] += 1
                dma("pool", dst, src, [], [wid], ("cv", cvn[0] % 8))

    def convert_layer(l, defer=False):
        RB = 128

        def cv(dst, src, wid):
            pending_cv.append((dst, src, wid))
            if not defer:
                pump(1)

        secs = [(6144, 10240), (2048, 6144), (0, 2048), (10240, 14336)]
        for si, (c0, c1) in enumerate(secs):
            for r0 in range(0, D, RB):
                cv(wb_in[l, r0:r0 + RB, c0:c1], w_in[l, r0:r0 + RB, c0:c1], ("cw_in", l, si, r0))
            if si == 0:
                for grp in range(4):
                    for gi in range(2):
                        for n in range(2):
                            for k in range(2):
                                o = ((gi * 2 + n) * 2 + k) * 256
                                cv(wb_rg[l, grp, :, o:o + 256], w_rg[l, gi, grp * 2 + n, k * 128:(k + 1) * 128, :],
                                   ("cw_rg", l, grp, gi, n, k))
        for j in range(2):
            for r0 in range(0, D, RB):
                cv(wb_br[l, j, r0:r0 + RB, :], w_br[l, j, r0:r0 + RB, :], ("cw_br", l, j, r0))
        for r0 in range(0, D, RB):
            cv(wb_out[l, r0:r0 + RB, :], w_out[l, r0:r0 + RB, :], ("cw_out", l, r0))
        for r0 in range(0, D, RB):
            cv(wb_gu[l, r0:r0 + RB, :], w_gu[l, r0:r0 + RB, :], ("cw_gu", l, r0))
        for r0 in range(0, DFF, RB):
            cv(wb_dn[l, r0:r0 + RB, :], w_dn[l, r0:r0 + RB, :], ("cw_dn", l, r0))

    def in_sec(c0):
        return 0 if 6144 <= c0 < 10240 else (1 if 2048 <= c0 < 6144 else (2 if c0 < 2048 else 3))

    def wtile(src2d, kcn, ncols, dep, r0=0):
        cvn[1] += 1
        if cvn[1] % 3 == 0:
            pump(1)
        i = nxt("w", NW)
        view = wring[i][:, 0:kcn * ncols].rearrange("p (k n) -> p k n", n=ncols)
        deps = [dep + (r0 + k * 128,) for k in range(kcn)]
        dma("sp", view, src2d.rearrange("(k p) n -> p k n", p=128), deps, [("w", i)], ("w", i))
        return view, ("w", i)

    def derive(l):
        lam_init = 0.8 - 0.6 * math.exp(-0.3 * (l + layer0))
        P0 = l * NPP
        d0 = l * 40
        actf(der[:, d0:d0 + 16], pp[:, P0 + O_LAM:P0 + O_LAM + 16], AF.Exp, ["pp"], [("der", l)], scale=-1.0)
        actf(der[:, d0:d0 + 16], der[:, d0:d0 + 16], AF.Ln, [("der", l)], [("der", l)], bias=1.0)
        tsc("dve", der[:, d0 + 16:d0 + 32], der[:, d0:d0 + 16], -16.0, None, ALU.mult, None, [("der", l)], [("der", l)])
        tsc("dve", der[:, d0:d0 + 16], der[:, d0:d0 + 16], -8.0, None, ALU.mult, None, [("der", l)], [("der", l)])
        tsc("dve", der[:, d0 + 32:d0 + 34], pp[:, P0 + O_SUB:P0 + O_SUB + 2], 1.0 - lam_init, None, ALU.mult, None,
            ["pp", ("der", l)], [("der", l)])
        tt("dve", der[:, d0 + 36:d0 + 37], pp[:, P0 + O_DL:P0 + O_DL + 1], pp[:, P0 + O_DL + 1:P0 + O_DL + 2], ALU.mult,
           ["pp", ("der", l)], [("der", l)])
        tt("dve", der[:, d0 + 37:d0 + 38], pp[:, P0 + O_DL + 2:P0 + O_DL + 3], pp[:, P0 + O_DL + 3:P0 + O_DL + 4], ALU.mult,
           ["pp", ("der", l)], [("der", l)])
        p, pid = PS()
        mm(p[:, 0:2], ones_f[:], der[:, d0 + 36:d0 + 38], True, True, ["ones_f", ("der", l)], [pid])
        actf(der[:, d0 + 36:d0 + 38], p[:, 0:2], AF.Exp, [pid], [("der", l)])
        tt("dve", der[:, d0 + 38:d0 + 39], der[:, d0 + 37:d0 + 38], der[:, d0 + 36:d0 + 37], ALU.subtract,
           [("der", l)], [("der", l)])
        tsc("dve", der[:, d0 + 34:d0 + 35], der[:, d0 + 38:d0 + 39], -lam_init, None, ALU.add, None,
            [("der", l)], [("der", l)])

    def ppc(l, off, c):
        return pp[:, l * NPP + off + c:l * NPP + off + c + 1]

    STOP = int(os.environ.get("KSTOP", "99"))

    def block(l, tb):
        try:
            block_body(l, tb)
        except StopIteration:
            pass
        dst = outT if l == L - 1 else xs[l]
        for c in range(KC):
            dma("pool", dst[c * 128:(c + 1) * 128, tb * TB:(tb + 1) * TB], xf[c][:], [("xf", c)], [("xs", l, tb, c)], ("xst", c))

    def phase(k):
        sc.tag = "ph%d" % k
        if STOP < k:
            raise StopIteration

    def x_src(l, tb, c):
        src = xT_in if l == 0 else xs[l - 1]
        return src[c * 128:(c + 1) * 128, tb * TB:(tb + 1) * TB], ("xs", l - 1, tb, c)

    def load_xf(l, tb):
        for c in range(KC):
            s_ap, s_id = x_src(l, tb, c)
            dma("sp", xf[c][:], s_ap, [s_id], [("xf", c)], ("xl", c))

    def prefetch_xb(l, tb):
        for c in range(KC):
            s_ap, s_id = x_src(l, tb, c)
            t_, tid = FT()
            dma("act", t_[:, 0:TB], s_ap, [s_id], [tid], ("xp", c % 8))
            cp("dve", xb[c][:], t_[:, 0:TB], [tid], [("xb", c)])

    def block_body(l, tb):
        t0 = tb * TB
        phase(1)
        last_layer = (l == L - 1)
        DER = ("der", l)
        d0 = l * 40
        first = (l == 0 and tb == 0) or STOP < 8
        if first:
            load_xf(l, tb)
        dma("sp", cos_t[:], cos_in[:, t0:t0 + TB], [], ["cos"], "cos")
        dma("sp", sin_t[:], sin_in[:, t0:t0 + TB], [], ["sin"], "sin")
        if first:
            for c in range(KC):
                cp("act" if c % 2 else "dve", xb[c][:], xf[c][:], [("xf", c)], [("xb", c)])

        xb_ids = [("xb", c) for c in range(KC)]

        def proj_fm(wv, wid, col, rhs, rhs_ids, nk):
            p, pid = PS()
            for kc in range(nk):
                mm(p[:], wv[:, kc, col * 128:(col + 1) * 128], rhs[kc][:], kc == 0, kc == nk - 1,
                   [wid, rhs_ids[kc]], [pid])
            return p, pid

        phase(2)

        def rnn_group(g):
            sc.tag = "ph2"
            wv, wid = wtile(wb_in[l, :, 6144 + g * 512:6144 + (g + 1) * 512], KC, 512, ("cw_in", l, 0))
            xc_t = []
            for j in range(4):
                c = g * 4 + j
                p, pid = proj_fm(wv, wid, j, xb, xb_ids, KC)
                x_t, x_id = FT()
                cp("dve", x_t[:, 0:3], carry[:, c * 3:c * 3 + 3], [("carry", c)], [x_id])
                actf(x_t[:, 3:3 + TB], p[:], AF.Copy, [pid], [x_id])
                cp("dve", carry[:, c * 3:c * 3 + 3], x_t[:, TB:TB + 3], [x_id], [("carry", c)])
                c_t, c_id = xcp[j], ("xc", j)
                tsc("dve", c_t[:, 0:TB], x_t[:, 0:TB], ppc(l, O_CW, 0 * 16 + c), ppc(l, O_CB, c), ALU.mult, ALU.add,
                    [x_id, "pp"], [c_id])
                for w in range(1, 4):
                    stt("dve", c_t[:, 0:TB], x_t[:, w:w + TB], ppc(l, O_CW, w * 16 + c), c_t[:, 0:TB], ALU.mult, ALU.add,
                        [x_id, c_id, "pp"], [c_id])
                b_t, b_id = BT()
                cp("act", b_t[:], c_t[:, 0:TB], [c_id], [b_id])
                xc_t.append((c_t, c_id, b_t, b_id))
            wv, wid = wtile(wb_in[l, :, 8192 + g * 512:8192 + (g + 1) * 512], KC, 512, ("cw_in", l, 0))
            for j in range(4):
                c = g * 4 + j
                p, pid = proj_fm(wv, wid, j, xb, xb_ids, KC)
                g_t, g_id = FT()
                a_t, a_id = FT()
                cp("act", g_t[:, 0:TB], p[:], [pid], [g_id])
                actf(a_t[:, 0:TB], g_t[:, 0:TB], AF.Square, [g_id], [a_id])
                tsc("dve", a_t[:, 0:TB], a_t[:, 0:TB], 0.044715, 1.0, ALU.mult, ALU.add, [a_id], [a_id])
                tt("dve", a_t[:, 0:TB], a_t[:, 0:TB], g_t[:, 0:TB], ALU.mult, [a_id, g_id], [a_id])
                actf(a_t[:, 0:TB], a_t[:, 0:TB], AF.Sigmoid, [a_id], [a_id], scale=GELU_C)
                tt("dve", hr[c][:], a_t[:, 0:TB], g_t[:, 0:TB], ALU.mult, [a_id, g_id], [("hr", c)])
            wi = nxt("w", NW)
            wrg_sb = wring[wi]
            dma("sp", wrg_sb[:, 0:2048], wb_rg[l, g], [("cw_rg", l, g, gi, n, k) for gi in range(2) for n in range(2) for k in range(2)],
                [("w", wi)], ("w", wi))
            for j in range(4):
                c = g * 4 + j
                n, m = j // 2, c % 2
                pr, prid = PS()
                pi, piid = PS()
                for gi, (pp_, ppid) in enumerate(((pr, prid), (pi, piid))):
                    for kc in range(2):
                        base = ((gi * 2 + n) * 2 + kc) * 256 + m * 128
                        bsrc = xc_t[n * 2 + kc]
                        mm(pp_[:], wrg_sb[:, base:base + 128], bsrc[2][:], kc == 0, kc == 1,
                           [("w", wi), bsrc[3]], [ppid])
                c_t, c_id = xc_t[j][0], xc_t[j][1]
                r_t, r_id = FT()
                i_t, i_id = FT()
                actf(r_t[:, 0:TB], pr[:], AF.Sigmoid, [prid, "pp"], [r_id], bias=ppc(l, O_BRG, c))
                actf(i_t[:, 0:TB], pi[:], AF.Sigmoid, [piid, "pp"], [i_id], bias=ppc(l, O_BRG, 16 + c))
                a_t, a_id = FT()
                actf(a_t[:, 0:TB], r_t[:, 0:TB], AF.Exp, [r_id, DER], [a_id], scale=der[:, d0 + c:d0 + c + 1])
                actf(r_t[:, 0:TB], r_t[:, 0:TB], AF.Exp, [r_id, DER], [r_id], scale=der[:, d0 + 16 + c:d0 + 17 + c])
                actf(r_t[:, 0:TB], r_t[:, 0:TB], AF.Ln, [r_id], [r_id], scale=-1.0, bias=1.0)
                actf(r_t[:, 0:TB], r_t[:, 0:TB], AF.Exp, [r_id], [r_id], scale=0.5)
                tt("dve", i_t[:, 0:TB], i_t[:, 0:TB], c_t[:, 0:TB], ALU.mult, [i_id, c_id], [i_id])
                tt("dve", i_t[:, 0:TB], i_t[:, 0:TB], r_t[:, 0:TB], ALU.mult, [i_id, r_id], [i_id])
                sc.add("dve", lambda e, o=r_t[:, 0:TB], a=a_t[:, 0:TB], u=i_t[:, 0:TB], h0=hstate[:, c:c + 1]:
                       e.tensor_tensor_scan(o, a, u, h0, ALU.mult, ALU.add),
                       reads=[a_id, i_id, ("hs", c)], writes=[r_id])
                cp("dve", hstate[:, c:c + 1], r_t[:, TB - 1:TB], [r_id], [("hs", c)])
                tt("dve", hr[c][:], r_t[:, 0:TB], hr[c][:], ALU.mult, [r_id, ("hr", c)], [("hr", c)])

        rope_pend = []

        def rope_flush(keep=0):
            while len(rope_pend) > keep:
                p, pid, b_t, b_id, out_ap, out_id, after = rope_pend.pop(0)
                p2, p2id = PS()
                mm(p2[:], perm[:], b_t[:], True, True, ["perm", b_id], [p2id])
                f1, f1id = FT()
                f2, f2id = FT()
                tt("dve", f1[:, 0:TB], p[:], cos_t[:], ALU.mult, [pid, "cos"], [f1id])
                tt("dve", f2[:, 0:TB], p2[:], sin_t[:], ALU.mult, [p2id, "sin"], [f2id])
                tt("dve", out_ap, f1[:, 0:TB], f2[:, 0:TB], ALU.add, [f1id, f2id], [out_id])
                if after is not None:
                    after()

        def rope_epilogue(p, pid, out_ap, out_id, after=None):
            b_t, b_id = BT()
            cp("act", b_t[:], p[:], [pid], [b_id])
            rope_pend.append((p, pid, b_t, b_id, out_ap, out_id, after))

        def k_group(g):
            sc.tag = "ph3"
            wv, wid = wtile(wb_in[l, :, 2048 + g * 512:2048 + (g + 1) * 512], KC, 512, ("cw_in", l, 1))
            si = nxt("kst", 1)

            def k_store(g=g, si=si):
                dma("pool", kT_s[l, g * 512:(g + 1) * 512, t0:t0 + TB].rearrange("(c p) t -> p c t", p=128),
                    kst[si][:, :].rearrange("p (c t) -> p c t", t=TB), [("kst", si)], [("kT", l, g, tb)], ("kst", si))

            for j in range(4):
                p, pid = proj_fm(wv, wid, j, xb, xb_ids, KC)
                rope_flush(keep=2)
                rope_epilogue(p, pid, kst[si][:, j * TB:(j + 1) * TB], ("kst", si), after=(k_store if j == 3 else None))

        def v_group(g):
            sc.tag = "ph3v"
            wv, wid = wtile(wb_in[l, :, 4096 + g * 512:4096 + (g + 1) * 512], KC, 512, ("cw_in", l, 1))
            si = nxt("kst", 1)
            for ts in range(4):
                p, pid = PS()
                for kc in range(KC):
                    mm(p[:], xb[kc][:, ts * 128:(ts + 1) * 128], wv[:, kc, :], kc == 0, kc == KC - 1,
                       [wid, ("xb", kc)], [pid])
                cp("act" if ts % 2 else "dve", kst[si][:, ts * TB:(ts + 1) * TB], p[:], [pid], [("kst", si)])
            dma("pool", v_s[l, t0:t0 + TB, g * 512:(g + 1) * 512].rearrange("(s p) e -> p s e", p=128),
                kst[si][:, :].rearrange("p (s e) -> p s e", e=512), [("kst", si)], [("v", l, g, tb)], ("kst", si))

        def q_group(g):
            sc.tag = "ph4"
            wv, wid = wtile(wb_in[l, :, g * 512:(g + 1) * 512], KC, 512, ("cw_in", l, 2))
            for j in range(4):
                c = g * 4 + j
                p, pid = proj_fm(wv, wid, j, xb, xb_ids, KC)
                rope_flush(keep=2)
                rope_epilogue(p, pid, qb[c][:], ("qb", c))

        for g in range(4):
            rnn_group(g)
            if g == 2 and not first:
                load_xf(l, tb)
            k_group(g)
            q_group(g)
            rope_flush()
            v_group(g)

        phase(5)
        S1, S2 = (ps[0], ("ps", 0)), (ps[1], ("ps", 1))
        OA = [[(ps[2], ("ps", 2)), (ps[3], ("ps", 3))], [(ps[4], ("ps", 4)), (ps[5], ("ps", 5))]]
        LS = [(ps[6], ("ps", 6)), (ps[7], ("ps", 7))]
        scale = HD ** -0.5
        nchunks = 4 * (tb + 1)
        kv_tiles = {}

        def load_kv(h, kb):
            i = nxt("kv", NKV)
            dma("pool", Kt[i][:, :].rearrange("p (c t) -> p c t", t=TB),
                kT_s[l, h * 256:(h + 1) * 256, kb * TB:(kb + 1) * TB].rearrange("(c p) t -> p c t", p=128),
                [("kT", l, h // 2, kb)], [("Kt", i)], ("Kt", i))
            dma("pool", Vt[i][:, :].rearrange("p (s e) -> p s e", e=256),
                v_s[l, kb * TB:(kb + 1) * TB, h * 256:(h + 1) * 256].rearrange("(s p) e -> p s e", p=128),
                [("v", l, h // 2, kb)], [("Vt", i)], ("Vt", i))
            kv_tiles[(h, kb)] = i

        def scores(h, j):
            kb, ks = j // 4, j % 4
            if (h, kb) not in kv_tiles:
                load_kv(h, kb)
            i = kv_tiles[(h, kb)]
            q0 = 128 * ks if kb == tb else 0
            for cc, (sp_, spid) in enumerate((S1, S2)):
                mm(sp_[:, q0:TB], Kt[i][:, cc * TB + ks * 128:cc * TB + (ks + 1) * 128], qb[2 * h + cc][:, q0:TB], True, True,
                   [("Kt", i), ("qb", 2 * h + cc)], [spid])
                e_i = (j % 2) * 2 + cc
                actf(Et[e_i][:, q0:TB], sp_[:, q0:TB], AF.Exp, [spid], [("Et", e_i)], scale=scale)
                if kb == tb:
                    tt("dve", Et[e_i][:, q0:TB], Et[e_i][:, q0:TB], masks[:, 384 - 128 * ks + q0:384 - 128 * ks + TB], ALU.mult,
                       [("Et", e_i), "masks"], [("Et", e_i)])

        def av(h, j):
            kb, ks = j // 4, j % 4
            i = kv_tiles[(h, kb)]
            q0 = 128 * ks if kb == tb else 0
            st, sp = (j == 0), (j == nchunks - 1)
            for cc in range(2):
                e_i = (j % 2) * 2 + cc
                for ec in range(2):
                    o_, oid = OA[cc][ec]
                    mm(o_[:, q0:TB], Vt[i][:, ks * 256 + ec * 128:ks * 256 + (ec + 1) * 128], Et[e_i][:, q0:TB], st, sp,
                       [("Vt", i), ("Et", e_i)], [oid])
                l_, lid = LS[cc]
                mm(l_[:, q0:TB], ones_bf[:], Et[e_i][:, q0:TB], st, sp, ["ones_bf", ("Et", e_i)], [lid])

        def epilogue_a(h):
            cpy = [None] * 4
            for ec in range(2):
                c_t, c_id = FT()
                cp("act", c_t[:, 0:TB], OA[0][ec][0][:], [OA[0][ec][1]], [c_id])
                cpy[ec] = (c_t, c_id)
            for ec in range(2):
                c_t, c_id = FT()
                cp("dve", c_t[:, 0:TB], OA[1][ec][0][:], [OA[1][ec][1]], [c_id])
                cpy[2 + ec] = (c_t, c_id)
            rl = []
            for cc in range(2):
                r_t, r_id = FT()
                recip(r_t[:, 0:TB], LS[cc][0][:], [LS[cc][1]], [r_id])
                rl.append((r_t, r_id))
            od = []
            for ec in range(2):
                a_t, a_id = cpy[ec]
                b_t, b_id = cpy[2 + ec]
                tt("dve", a_t[:, 0:TB], a_t[:, 0:TB], rl[0][0][:, 0:TB], ALU.mult, [a_id, rl[0][1]], [a_id])
                tt("dve", b_t[:, 0:TB], b_t[:, 0:TB], rl[1][0][:, 0:TB], ALU.mult, [b_id, rl[1][1]], [b_id])
                stt("dve", a_t[:, 0:TB], b_t[:, 0:TB], der[:, d0 + 34:d0 + 35], a_t[:, 0:TB], ALU.mult, ALU.add,
                    [a_id, b_id, DER], [a_id])
                q_t, q_id = BT()
                tt("dve", q_t[:], a_t[:, 0:TB], a_t[:, 0:TB], ALU.mult, [a_id], [q_id])
                od.append((a_t, a_id, q_t, q_id))
            return od

        def epilogue_b(h, od):
            p, pid = S1
            for ec in range(2):
                mm(p[:], ones_sub[:], od[ec][2][:], ec == 0, ec == 1, ["ones_sub", od[ec][3]], [pid])
            r_t, r_id = FT()
            actf(r_t[:, 0:TB], p[:], AF.Ln, [pid], [r_id], bias=EPS)
            actf(r_t[:, 0:TB], r_t[:, 0:TB], AF.Exp, [r_id], [r_id], scale=-0.5)
            for ec in range(2):
                stt("dve", ob[2 * h + ec][:], od[ec][0][:, 0:TB], der[:, d0 + 32 + ec:d0 + 33 + ec], r_t[:, 0:TB],
                    ALU.mult, ALU.mult, [od[ec][1], r_id, DER], [("ob", 2 * h + ec)])

        pend = None
        jdef = min(3, nchunks - 1)
        scores(0, 0)
        for h in range(NH):
            for j in range(nchunks):
                if j + 1 < nchunks:
                    scores(h, j + 1)
                av(h, j)
                if j == jdef and pend is not None:
                    epilogue_b(*pend)
                    pend = None
            if h + 1 < NH:
                scores(h + 1, 0)
            od = epilogue_a(h)
            pend = (h, od)
        epilogue_b(*pend)

        phase(6)
        mg = qb
        ob_ids = [("ob", c) for c in range(KC)]
        hr_ids = [("hr", c) for c in range(KC)]
        for g in range(4):
            gts = []
            for which in range(2):
                wv, wid = wtile(wb_in[l, :, 10240 + which * 2048 + g * 512:10240 + which * 2048 + (g + 1) * 512],
                                KC, 512, ("cw_in", l, 3))
                for j in range(4):
                    c = g * 4 + j
                    p, pid = proj_fm(wv, wid, j, xb, xb_ids, KC)
                    g_t, g_id = FT()
                    actf(g_t[:, 0:TB], p[:], AF.Sigmoid, [pid, "pp"], [g_id], bias=ppc(l, O_BM, which * 16 + c))
                    gts.append((g_t, g_id))
            wv, wid = wtile(wb_br[l, 0, :, g * 512:(g + 1) * 512], KC, 512, ("cw_br", l, 0))
            for j in range(4):
                p, pid = proj_fm(wv, wid, j, ob, ob_ids, KC)
                g_t, g_id = gts[j]
                tt("dve", g_t[:, 0:TB], g_t[:, 0:TB], p[:], ALU.mult, [g_id, pid], [g_id])
            wv, wid = wtile(wb_br[l, 1, :, g * 512:(g + 1) * 512], KC, 512, ("cw_br", l, 1))
            for j in range(4):
                c = g * 4 + j
                p, pid = proj_fm(wv, wid, j, hr, hr_ids, KC)
                g_t, g_id = gts[4 + j]
                tt("dve", g_t[:, 0:TB], g_t[:, 0:TB], p[:], ALU.mult, [g_id, pid], [g_id])
                tt("dve", mg[c][:], g_t[:, 0:TB], gts[j][0][:, 0:TB], ALU.add, [g_id, gts[j][1]], [("qb", c)])

        def layer_norm(j, make_bf):
            pm, pmid = PS()
            pq, pqid = PS()
            for c in range(KC):
                b1, b1id = BT()
                b2, b2id = BT()
                cp("act", b1[:], xf[c][:], [("xf", c)], [b1id])
                actf(b2[:], xf[c][:], AF.Square, [("xf", c)], [b2id])
                mm(pm[:], ones_ln[:], b1[:], c == 0, c == KC - 1, ["ones_ln", b1id], [pmid])
                mm(pq[:], ones_ln[:], b2[:], c == 0, c == KC - 1, ["ones_ln", b2id], [pqid])
            mu, muid = FT()
            rs, rsid = FT()
            cp("dve", mu[:, 0:TB], pm[:], [pmid], [muid])
            tt("dve", rs[:, 0:TB], mu[:, 0:TB], mu[:, 0:TB], ALU.mult, [muid], [rsid])
            tt("dve", rs[:, 0:TB], pq[:], rs[:, 0:TB], ALU.subtract, [pqid, rsid], [rsid])
            actf(rs[:, 0:TB], rs[:, 0:TB], AF.Ln, [rsid], [rsid], bias=EPS)
            actf(rs[:, 0:TB], rs[:, 0:TB], AF.Exp, [rsid], [rsid], scale=-0.5)
            for c in range(KC):
                tt("dve", xf[c][:], xf[c][:], mu[:, 0:TB], ALU.subtract, [("xf", c), muid], [("xf", c)])
                tt("dve", xf[c][:], xf[c][:], rs[:, 0:TB], ALU.mult, [("xf", c), rsid], [("xf", c)])
                if make_bf:
                    actf(xb[c][:], xf[c][:], AF.Identity, [("xf", c), "pp"], [("xb", c)],
                         bias=ppc(l, O_LNB, j * 16 + c), scale=ppc(l, O_LNG, j * 16 + c))
                actf(xf[c][:], xf[c][:], AF.Identity, [("xf", c), "pp"], [("xf", c)],
                     bias=ppc(l, O_LNB, j * 16 + c), scale=ppc(l, O_LNG, j * 16 + c))


        phase(7)
        mg_ids = [("qb", c) for c in range(KC)]
        for g in range(4):
            wv, wid = wtile(wb_out[l, :, g * 512:(g + 1) * 512], KC, 512, ("cw_out", l))
            for j in range(4):
                c = g * 4 + j
                p, pid = proj_fm(wv, wid, j, mg, mg_ids, KC)
                stt("dve", xf[c][:], xf[c][:], ALPHA, p[:], ALU.mult, ALU.add, [("xf", c), pid], [("xf", c)])
        layer_norm(0, True)

        phase(8)
        for g in range(11):
            wv, wid = wtile(wb_gu[l, :, g * 512:(g + 1) * 512], KC, 512, ("cw_gu", l))
            sg = []
            for j in range(4):
                p, pid = proj_fm(wv, wid, j, xb, xb_ids, KC)
                s_t, s_id = FT()
                actf(s_t[:, 0:TB], p[:], AF.Silu, [pid], [s_id])
                sg.append((s_t, s_id))
            wv, wid = wtile(wb_gu[l, :, DFF + g * 512:DFF + (g + 1) * 512], KC, 512, ("cw_gu", l))
            for j in range(4):
                c = g * 4 + j
                p, pid = proj_fm(wv, wid, j, xb, xb_ids, KC)
                tt("dve", act[c][:], sg[j][0][:, 0:TB], p[:], ALU.mult, [sg[j][1], pid], [act_ids[c]])
        nb = (l, tb + 1) if tb + 1 < NTB else ((l + 1, 0) if l + 1 < L else None)
        for g in range(8):
            if g == 2 and nb is not None:
                prefetch_xb(*nb)
            pa = [PS(), PS()]
            for half in range(2):
                wv, wid = wtile(wb_dn[l, half * 2816:(half + 1) * 2816, g * 256:(g + 1) * 256], 22, 256, ("cw_dn", l), r0=half * 2816)
                for j in range(2):
                    for kc in range(22):
                        k = half * 22 + kc
                        mm(pa[j][0][:], wv[:, kc, j * 128:(j + 1) * 128], act[k][:], k == 0, k == FC - 1,
                           [wid, act_ids[k]], [pa[j][1]])
            for j in range(2):
                c = g * 2 + j
                stt("dve", xf[c][:], xf[c][:], ALPHA, pa[j][0][:], ALU.mult, ALU.add, [("xf", c), pa[j][1]], [("xf", c)])
        layer_norm(1, False)

    convert_layer(0)
    for l in range(L):
        pump(len(pending_cv))
        if l + 1 < L:
            convert_layer(l + 1, defer=True)
        derive(l)
        memset("dve", carry[:], 0.0, [("carry", c) for c in range(KC)])
        memset("dve", hstate[:], 0.0, [("hs", c) for c in range(KC)])
        for tb in range(NTB):
            block(l, tb)
    sc.add("pool", lambda e: e.engine_nop(), reads=[("xs", L - 1, tb, c) for tb in range(NTB) for c in range(KC)],
           writes=["final"])
    sc.emit(nc, stack)
    stack.close()
    nc._sched = sc
    return nc


def _pack_pp(inp, layers):
    cols = []
    for l in layers:
        def fm(a):
            a = np.asarray(a, np.float32).reshape(-1, KC, 128)
            return np.ascontiguousarray(a.transpose(2, 0, 1).reshape(128, -1))
        blk = [fm(inp["b_merge"][l]), fm(inp["conv_w"][l]), fm(inp["conv_b"][l]), fm(inp["b_rg"][l]),
               fm(inp["lru_lambda"][l]), fm(inp["ln_g"][l]), fm(inp["ln_b"][l]),
               np.ascontiguousarray(np.asarray(inp["subln_g"][l], np.float32).reshape(2, 128).T),
               np.ascontiguousarray(np.asarray(inp["diff_lambda"][l], np.float32).T)]
        blk = np.concatenate(blk, axis=1)
        assert blk.shape == (128, NPP), blk.shape
        cols.append(blk)
    return np.ascontiguousarray(np.concatenate(cols, axis=1))


def _consts(S):
    import ml_dtypes
    half = HD // 2
    inv = (10000.0 ** (-np.arange(half, dtype=np.float32) * 2.0 / HD)).astype(np.float32)
    ang = np.arange(S, dtype=np.float32)[:, None] * inv[None, :]
    ang = np.concatenate([ang, ang], axis=-1)
    cosT = np.ascontiguousarray(np.cos(ang).T.astype(np.float32))
    sgn = np.where(np.arange(HD) < half, -1.0, 1.0).astype(np.float32)
    sinS = np.ascontiguousarray((np.sin(ang) * sgn[None, :]).T.astype(np.float32))
    perm = np.zeros((128, 128), np.float32)
    for d2 in range(128):
        perm[(d2 + 64) % 128, d2] = 1.0
    k = np.arange(128)[:, None]
    xx = np.arange(TB + 384)[None, :]
    masks = ((xx - 384) >= k).astype(np.float32)
    return cosT, sinS, perm.astype(ml_dtypes.bfloat16), masks.astype(ml_dtypes.bfloat16)


_PROG_CACHE = {}


def _run(xT_list, inp, layers, S, layer0):
    key = (len(layers), S, layer0)
    if key not in _PROG_CACHE:
        _PROG_CACHE[key] = build_program(len(layers), S, layer0=layer0)
    nc = _PROG_CACHE[key]
    cosT, sinS, perm, masks = _consts(S)
    sl = slice(layers[0], layers[-1] + 1)
    shared = {
        "w_in": np.ascontiguousarray(inp["w_in"][sl]), "w_rg": np.ascontiguousarray(inp["w_rg"][sl]),
        "w_branch": np.ascontiguousarray(inp["w_branch"][sl]), "w_out": np.ascontiguousarray(inp["w_out"][sl]),
        "w_gate_up": np.ascontiguousarray(inp["w_gate_up"][sl]), "w_down": np.ascontiguousarray(inp["w_down"][sl]),
        "pp": _pack_pp(inp, layers), "cosT": cosT, "sinS": sinS, "perm": perm, "masks": masks,
    }
    in_maps = [dict(shared, xT=xT) for xT in xT_list]
    res = run_bass_kernel_spmd(nc, in_maps, core_ids=list(range(len(xT_list))))
    return [np.asarray(r["outT"]) for r in res.results]


def kernel(**inputs):
    inp = {k: np.asarray(v) for k, v in inputs.items()}
    x = inp["x"].astype(np.float32, copy=False)
    B, S, _ = x.shape
    xT = [np.ascontiguousarray(x[b].T) for b in range(B)]
    outT = _run(xT, inp, list(range(DEPTH)), S, 0)
    return np.stack([o.T for o in outT], axis=0).astype(np.float32)
```

```python
import math
import os
from contextlib import ExitStack

import numpy as np
import concourse.bass as bass
import concourse.mybir as mybir
from concourse.bass_utils import run_bass_kernel_spmd

F32 = mybir.dt.float32
BF16 = mybir.dt.bfloat16
AF = mybir.ActivationFunctionType
ALU = mybir.AluOpType

D = 2048
KC = 16
HD = 128
NH = 8
DFF = 5632
FC = 44
N_IN = 14336
TB = 512
EPS = 1e-5
DEPTH = 4
ALPHA = (2.0 * DEPTH) ** 0.25
NPP = 230
O_BM, O_CW, O_CB, O_BRG, O_LAM, O_LNG, O_LNB, O_SUB, O_DL = 0, 32, 96, 112, 144, 160, 192, 224, 226
GELU_C = 2.0 * math.sqrt(2.0 / math.pi)


class _Op:
    __slots__ = ("eng", "fn", "deps", "dom", "sig", "seq", "dma", "tag")

    def __init__(self, eng, fn, deps, dom, dma):
        self.eng, self.fn, self.deps, self.dom, self.dma = eng, fn, deps, dom, dma
        self.sig = False
        self.seq = 0


class Sched:
    def __init__(self):
        self.ops = []
        self.lastw = {}
        self.readers = {}
        self.dma_last = {}
        self.tag = ""
        self.gen = {}

    def _norm(self, bufs):
        out = []
        for b in bufs:
            if isinstance(b, tuple) and len(b) == 3 and b[0] in ("ft", "bt"):
                assert self.gen[b[:2]] == b[2], ("stale temp tile", b, self.gen[b[:2]], self.tag)
                b = b[:2]
            out.append(b)
        return out

    def add(self, eng, fn, reads=(), writes=(), dma_key=None):
        idx = len(self.ops)
        deps = set()
        reads = self._norm(reads)
        writes = self._norm(writes)
        ex = [b for b in reads if isinstance(b, tuple) and b[0] == "ps"]
        if ex:
            reads = [b for b in reads if b not in ex]
            writes = list(writes) + ex
        for b in reads:
            w = self.lastw.get(b)
            if w is not None:
                deps.add(w)
        for b in writes:
            w = self.lastw.get(b)
            if w is not None:
                deps.add(w)
            rd = self.readers.get(b)
            if rd:
                deps.update(rd.values())
        dom = ("dma", dma_key) if dma_key is not None else eng
        if dma_key is not None:
            p = self.dma_last.get(dma_key)
            if p is not None:
                deps.add(p)
            self.dma_last[dma_key] = idx
        for b in reads:
            self.readers.setdefault(b, {})[dom] = idx
        for b in writes:
            self.lastw[b] = idx
            self.readers[b] = {}
        deps.discard(idx)
        if eng == "pe":
            deps = {d for d in deps if self.ops[d].eng != "pe"}
        op = _Op(eng, fn, deps, dom, dma_key is not None)
        op.tag = self.tag
        self.ops.append(op)
        return idx

    def emit(self, nc, stack):
        ops = self.ops
        for op in ops:
            if op.dma:
                op.sig = True
            for d in op.deps:
                ops[d].sig = True
        cnt = {}
        for op in ops:
            if op.sig:
                cnt[op.dom] = cnt.get(op.dom, 0) + 1
                op.seq = cnt[op.dom]
        sems = {}
        for i, dom in enumerate(cnt):
            sems[dom] = stack.enter_context(nc.semaphore("s%d" % i))
        per_eng = {}
        for i, op in enumerate(ops):
            per_eng.setdefault(op.eng, []).append(i)
        block = stack.enter_context(nc.Block())

        def run(eng_name, e):
            known = {}
            for i in per_eng.get(eng_name, []):
                op = ops[i]
                need = {}
                for d in op.deps:
                    o = ops[d]
                    if o.seq > need.get(o.dom, 0):
                        need[o.dom] = o.seq
                for dom, seq in need.items():
                    if known.get(dom, 0) >= seq:
                        continue
                    known[dom] = seq
                    unit = 16 if dom[0] == "dma" else 1
                    e.wait_ge(sems[dom], seq * unit)
                ins = op.fn(e)
                if op.sig:
                    ins.then_inc(sems[op.dom], 16 if op.dma else 1)

        @block.tensor
        def _(e):
            run("pe", e)

        @block.scalar
        def _(e):
            run("act", e)

        @block.vector
        def _(e):
            run("dve", e)

        @block.gpsimd
        def _(e):
            run("pool", e)

        @block.sync
        def _(e):
            run("sp", e)


def build_program(n_layers, S, layer0=0, dbg=False):
    NTB = S // TB
    L = n_layers
    nc = bass.Bass("TRN2", target_bir_lowering=False)
    sc = Sched()
    stack = ExitStack()

    def dram_in(name, shape, dt=F32):
        return nc.dram_tensor(name, list(shape), dt, kind="ExternalInput").ap()

    def dram_tmp(name, shape, dt):
        return nc.dram_tensor(name, list(shape), dt, kind="Internal").ap()

    xT_in = dram_in("xT", [D, S])
    w_in = dram_in("w_in", [L, D, N_IN])
    w_rg = dram_in("w_rg", [L, 2, 8, 256, 256])
    w_br = dram_in("w_branch", [L, 2, D, D])
    w_out = dram_in("w_out", [L, D, D])
    w_gu = dram_in("w_gate_up", [L, D, 2 * DFF])
    w_dn = dram_in("w_down", [L, DFF, D])
    pp_in = dram_in("pp", [128, L * NPP])
    cos_in = dram_in("cosT", [128, S])
    sin_in = dram_in("sinS", [128, S])
    perm_in = dram_in("perm", [128, 128], BF16)
    mask_in = dram_in("masks", [128, TB + 384], BF16)
    outT = nc.dram_tensor("outT", [D, S], F32, kind="ExternalOutput").ap()

    wb_in = dram_tmp("wb_in", [L, D, N_IN], BF16)
    wb_rg = dram_tmp("wb_rg", [L, 4, 128, 2048], BF16)
    wb_br = dram_tmp("wb_br", [L, 2, D, D], BF16)
    wb_out = dram_tmp("wb_out", [L, D, D], BF16)
    wb_gu = dram_tmp("wb_gu", [L, D, 2 * DFF], BF16)
    wb_dn = dram_tmp("wb_dn", [L, DFF, D], BF16)
    kT_s = dram_tmp("kT_s", [L, D, S], BF16)
    v_s = dram_tmp("v_s", [L, S, D], BF16)
    xs = dram_tmp("x_s", [max(L - 1, 1), D, S], F32)

    def sb(name, shape, dt):
        return stack.enter_context(nc.sbuf_tensor(name, list(shape), dt))

    def pst(name):
        return stack.enter_context(nc.psum_tensor(name, [128, 512], F32))

    NW = 3
    wring = [sb("wr%d" % i, [128, 8192], BF16) for i in range(NW)]
    pp = sb("pp_sb", [128, L * NPP], F32)
    der = sb("der", [128, L * 40], F32)
    ones_bf = sb("ones_bf", [128, 128], BF16)
    ones_ln = sb("ones_ln", [128, 128], BF16)
    ones_sub = sb("ones_sub", [128, 128], BF16)
    ones_f = sb("ones_f", [128, 128], F32)
    perm = sb("perm_sb", [128, 128], BF16)
    masks = sb("mask_sb", [128, TB + 384], BF16)
    cos_t = sb("cos_t", [128, TB], F32)
    sin_t = sb("sin_t", [128, TB], F32)
    xf = [sb("xf%d" % c, [128, TB], F32) for c in range(KC)]
    xb = [sb("xb%d" % c, [128, TB], BF16) for c in range(KC)]
    carry = sb("carry", [128, KC * 3], F32)
    hstate = sb("hstate", [128, KC], F32)
    NFT = 9
    ft = [sb("ft%d" % i, [128, 3 + TB], F32) for i in range(NFT)]
    xcp = [sb("xc%d" % i, [128, TB], F32) for i in range(4)]
    NBT = 8
    bt = [sb("bt%d" % i, [128, TB], BF16) for i in range(NBT)]
    hr = [sb("hr%d" % c, [128, TB], BF16) for c in range(KC)]
    qb = [sb("qb%d" % c, [128, TB], BF16) for c in range(KC)]
    ob = [sb("ob%d" % c, [128, TB], BF16) for c in range(KC)]
    kst = [sb("kst%d" % i, [128, 4 * TB], BF16) for i in range(1)]
    NKV = 2
    Kt = [sb("Kt%d" % i, [128, 2 * TB], BF16) for i in range(NKV)]
    Vt = [sb("Vt%d" % i, [128, 4 * 256], BF16) for i in range(NKV)]
    Et = [sb("Et%d" % i, [128, TB], BF16) for i in range(4)]
    act = hr + ob + qb[:FC - 2 * KC]
    act_ids = [("hr", c) for c in range(KC)] + [("ob", c) for c in range(KC)] + [("qb", c) for c in range(FC - 2 * KC)]
    ps = [pst("ps%d" % i) for i in range(8)]

    ctr = {"ft": 0, "bt": 0, "ps": 0, "w": 0, "kv": 0, "kst": 0}

    def nxt(kind, n):
        i = ctr[kind]
        ctr[kind] = (i + 1) % n
        return i

    gens = {"n": 0}

    def FT():
        i = nxt("ft", NFT)
        gens["n"] += 1
        sc.gen[("ft", i)] = gens["n"]
        return ft[i], ("ft", i, gens["n"])

    def BT():
        i = nxt("bt", NBT)
        gens["n"] += 1
        sc.gen[("bt", i)] = gens["n"]
        return bt[i], ("bt", i, gens["n"])

    def PS():
        i = nxt("ps", 8)
        return ps[i], ("ps", i)

    def dma(eng, out, in_, reads, writes, key):
        sc.add(eng, lambda e: e.dma_start(out=out, in_=in_), reads=reads, writes=writes, dma_key=key)

    def mm(out, lhsT, rhs, start, stop, reads, writes):
        sc.add("pe", lambda e: e.matmul(out, lhsT, rhs, start=start, stop=stop), reads=reads, writes=writes)

    def actf(out, in_, func, reads, writes, bias=None, scale=None):
        kw = {}
        if bias is not None:
            kw["bias"] = bias
        if scale is not None:
            kw["scale"] = scale
        sc.add("act", lambda e: e.activation(out, in_, func, **kw), reads=reads, writes=writes)

    def tt(eng, out, in0, in1, op, reads, writes):
        sc.add(eng, lambda e: e.tensor_tensor(out, in0, in1, op), reads=reads, writes=writes)

    def tsc(eng, out, in0, s1, s2, op0, op1, reads, writes):
        if op1 is None:
            sc.add(eng, lambda e: e.tensor_scalar(out, in0, s1, None, op0), reads=reads, writes=writes)
        else:
            sc.add(eng, lambda e: e.tensor_scalar(out, in0, s1, s2, op0, op1), reads=reads, writes=writes)

    def stt(eng, out, in0, scalar, in1, op0, op1, reads, writes):
        sc.add(eng, lambda e: e.scalar_tensor_tensor(out, in0, scalar, in1, op0, op1), reads=reads, writes=writes)

    def cp(eng, out, in_, reads, writes):
        if eng == "act":
            sc.add(eng, lambda e: e.activation(out, in_, AF.Copy), reads=reads, writes=writes)
        else:
            sc.add(eng, lambda e: e.tensor_copy(out, in_), reads=reads, writes=writes)

    def recip(out, in_, reads, writes):
        sc.add("dve", lambda e: e.reciprocal(out, in_), reads=reads, writes=writes)

    def memset(eng, ap, val, writes):
        sc.add(eng, lambda e: e.memset(ap, val), writes=writes)

    memset("dve", ones_bf[:], 1.0, ["ones_bf"])
    memset("dve", ones_ln[:], 1.0 / D, ["ones_ln"])
    memset("dve", ones_sub[:], 1.0 / 256.0, ["ones_sub"])
    memset("dve", ones_f[:], 1.0, ["ones_f"])
    dma("sp", pp[:], pp_in[:, :], [], ["pp"], "c0")
    dma("sp", perm[:], perm_in[:, :], [], ["perm"], "c1")
    dma("sp", masks[:], mask_in[:, :], [], ["masks"], "c2")

    pending_cv = []
    cvn = [0, 0]

    def pump(n):
        for _ in range(n):
            if pending_cv:
                dst, src, wid = pending_cv.pop(0)
                cvn[0] += 1
                dma("pool", dst, src, [], [wid], ("cv", cvn[0] % 8))

    CP = {}

    def convert_layer(l, defer=False):
        RB = 128

        def cv(dst, src, wid):
            pending_cv.append((dst, src, wid))
            if not defer:
                pump(1)

        def conv2d(name, dst, src, nrows, c0, c1, pw):
            pieces = CP.setdefault(name, [])
            for a0 in range(c0, c1, pw):
                a1 = min(a0 + pw, c1)
                if (a0, a1) not in pieces:
                    pieces.append((a0, a1))
                for r0 in range(0, nrows, RB):
                    cv(dst[r0:r0 + RB, a0:a1], src[r0:r0 + RB, a0:a1], ("cw", name, l, r0, a0))

        secs = [(6144, 10240), (2048, 6144), (0, 2048), (10240, 14336)]
        for si, (c0, c1) in enumerate(secs):
            conv2d("in", wb_in[l], w_in[l], D, c0, c1, 2048)
            if si == 0:
                for grp in range(4):
                    for gi in range(2):
                        for n in range(2):
                            for k in range(2):
                                o = ((gi * 2 + n) * 2 + k) * 256
                                cv(wb_rg[l, grp, :, o:o + 256], w_rg[l, gi, grp * 2 + n, k * 128:(k + 1) * 128, :],
                                   ("cw_rg", l, grp, gi, n, k))
        for j in range(2):
            conv2d("br%d" % j, wb_br[l, j], w_br[l, j], D, 0, D, 2048)
        conv2d("out", wb_out[l], w_out[l], D, 0, D, 2048)
        conv2d("gu", wb_gu[l], w_gu[l], D, 0, 2 * DFF, 2816)
        conv2d("dn", wb_dn[l], w_dn[l], DFF, 0, D, 2048)

    def in_sec(c0):
        return 0 if 6144 <= c0 < 10240 else (1 if 2048 <= c0 < 6144 else (2 if c0 < 2048 else 3))

    def wtile(src2d, kcn, ncols, dep, r0=0):
        cvn[1] += 1
        if cvn[1] % 2 == 0:
            pump(1)
        i = nxt("w", NW)
        view = wring[i][:, 0:kcn * ncols].rearrange("p (k n) -> p k n", n=ncols)
        name, l_, c0 = dep
        deps = []
        for (a0, a1) in CP[name]:
            if a0 < c0 + ncols and c0 < a1:
                deps += [("cw", name, l_, r0 + k * 128, a0) for k in range(kcn)]
        dma("sp", view, src2d.rearrange("(k p) n -> p k n", p=128), deps, [("w", i)], ("w", i))
        return view, ("w", i)

    def derive(l):
        lam_init = 0.8 - 0.6 * math.exp(-0.3 * (l + layer0))
        P0 = l * NPP
        d0 = l * 40
        actf(der[:, d0:d0 + 16], pp[:, P0 + O_LAM:P0 + O_LAM + 16], AF.Exp, ["pp"], [("der", l)], scale=-1.0)
        actf(der[:, d0:d0 + 16], der[:, d0:d0 + 16], AF.Ln, [("der", l)], [("der", l)], bias=1.0)
        tsc("dve", der[:, d0 + 16:d0 + 32], der[:, d0:d0 + 16], -16.0, None, ALU.mult, None, [("der", l)], [("der", l)])
        tsc("dve", der[:, d0:d0 + 16], der[:, d0:d0 + 16], -8.0, None, ALU.mult, None, [("der", l)], [("der", l)])
        tsc("dve", der[:, d0 + 32:d0 + 34], pp[:, P0 + O_SUB:P0 + O_SUB + 2], 1.0 - lam_init, None, ALU.mult, None,
            ["pp", ("der", l)], [("der", l)])
        tt("dve", der[:, d0 + 36:d0 + 37], pp[:, P0 + O_DL:P0 + O_DL + 1], pp[:, P0 + O_DL + 1:P0 + O_DL + 2], ALU.mult,
           ["pp", ("der", l)], [("der", l)])
        tt("dve", der[:, d0 + 37:d0 + 38], pp[:, P0 + O_DL + 2:P0 + O_DL + 3], pp[:, P0 + O_DL + 3:P0 + O_DL + 4], ALU.mult,
           ["pp", ("der", l)], [("der", l)])
        p, pid = PS()
        mm(p[:, 0:2], ones_f[:], der[:, d0 + 36:d0 + 38], True, True, ["ones_f", ("der", l)], [pid])
        actf(der[:, d0 + 36:d0 + 38], p[:, 0:2], AF.Exp, [pid], [("der", l)])
        tt("dve", der[:, d0 + 38:d0 + 39], der[:, d0 + 37:d0 + 38], der[:, d0 + 36:d0 + 37], ALU.subtract,
           [("der", l)], [("der", l)])
        tsc("dve", der[:, d0 + 34:d0 + 35], der[:, d0 + 38:d0 + 39], -lam_init, None, ALU.add, None,
            [("der", l)], [("der", l)])

    def ppc(l, off, c):
        return pp[:, l * NPP + off + c:l * NPP + off + c + 1]

    STOP = int(os.environ.get("KSTOP", "99"))

    def block(l, tb):
        try:
            block_body(l, tb)
        except StopIteration:
            pass
        dst = outT if l == L - 1 else xs[l]
        for c in range(KC):
            dma("pool", dst[c * 128:(c + 1) * 128, tb * TB:(tb + 1) * TB], xf[c][:], [("xf", c)], [("xs", l, tb, c)], ("xst", c))

    def phase(k):
        sc.tag = "ph%d" % k
        if STOP < k:
            raise StopIteration

    def x_src(l, tb, c):
        src = xT_in if l == 0 else xs[l - 1]
        return src[c * 128:(c + 1) * 128, tb * TB:(tb + 1) * TB], ("xs", l - 1, tb, c)

    def load_xf(l, tb):
        for c in range(KC):
            s_ap, s_id = x_src(l, tb, c)
            dma("sp", xf[c][:], s_ap, [s_id], [("xf", c)], ("xl", c))

    def prefetch_xb(l, tb):
        for c in range(KC):
            s_ap, s_id = x_src(l, tb, c)
            t_, tid = FT()
            dma("act", t_[:, 0:TB], s_ap, [s_id], [tid], ("xp", c % 8))
            cp("dve", xb[c][:], t_[:, 0:TB], [tid], [("xb", c)])

    def block_body(l, tb):
        t0 = tb * TB
        phase(1)
        last_layer = (l == L - 1)
        DER = ("der", l)
        d0 = l * 40
        first = (l == 0 and tb == 0) or STOP < 8
        if first:
            load_xf(l, tb)
        dma("sp", cos_t[:], cos_in[:, t0:t0 + TB], [], ["cos"], "cos")
        dma("sp", sin_t[:], sin_in[:, t0:t0 + TB], [], ["sin"], "sin")
        if first:
            for c in range(KC):
                cp("act" if c % 2 else "dve", xb[c][:], xf[c][:], [("xf", c)], [("xb", c)])

        xb_ids = [("xb", c) for c in range(KC)]

        def proj_fm(wv, wid, col, rhs, rhs_ids, nk):
            p, pid = PS()
            for kc in range(nk):
                mm(p[:], wv[:, kc, col * 128:(col + 1) * 128], rhs[kc][:], kc == 0, kc == nk - 1,
                   [wid, rhs_ids[kc]], [pid])
            return p, pid

        phase(2)

        def rnn_group(g):
            sc.tag = "ph2"
            wv, wid = wtile(wb_in[l, :, 6144 + g * 512:6144 + (g + 1) * 512], KC, 512, ("in", l, 6144 + g * 512))
            xc_t = []
            for j in range(4):
                c = g * 4 + j
                p, pid = proj_fm(wv, wid, j, xb, xb_ids, KC)
                x_t, x_id = FT()
                cp("dve", x_t[:, 0:3], carry[:, c * 3:c * 3 + 3], [("carry", c)], [x_id])
                actf(x_t[:, 3:3 + TB], p[:], AF.Copy, [pid], [x_id])
                cp("dve", carry[:, c * 3:c * 3 + 3], x_t[:, TB:TB + 3], [x_id], [("carry", c)])
                c_t, c_id = xcp[j], ("xc", j)
                tsc("dve", c_t[:, 0:TB], x_t[:, 0:TB], ppc(l, O_CW, 0 * 16 + c), ppc(l, O_CB, c), ALU.mult, ALU.add,
                    [x_id, "pp"], [c_id])
                for w in range(1, 4):
                    stt("dve", c_t[:, 0:TB], x_t[:, w:w + TB], ppc(l, O_CW, w * 16 + c), c_t[:, 0:TB], ALU.mult, ALU.add,
                        [x_id, c_id, "pp"], [c_id])
                b_t, b_id = BT()
                cp("act", b_t[:], c_t[:, 0:TB], [c_id], [b_id])
                xc_t.append((c_t, c_id, b_t, b_id))
            wv, wid = wtile(wb_in[l, :, 8192 + g * 512:8192 + (g + 1) * 512], KC, 512, ("in", l, 8192 + g * 512))
            for j in range(4):
                c = g * 4 + j
                p, pid = proj_fm(wv, wid, j, xb, xb_ids, KC)
                g_t, g_id = FT()
                a_t, a_id = FT()
                cp("act", g_t[:, 0:TB], p[:], [pid], [g_id])
                actf(a_t[:, 0:TB], g_t[:, 0:TB], AF.Square, [g_id], [a_id])
                tsc("dve", a_t[:, 0:TB], a_t[:, 0:TB], 0.044715, 1.0, ALU.mult, ALU.add, [a_id], [a_id])
                tt("dve", a_t[:, 0:TB], a_t[:, 0:TB], g_t[:, 0:TB], ALU.mult, [a_id, g_id], [a_id])
                actf(a_t[:, 0:TB], a_t[:, 0:TB], AF.Sigmoid, [a_id], [a_id], scale=GELU_C)
                tt("dve", hr[c][:], a_t[:, 0:TB], g_t[:, 0:TB], ALU.mult, [a_id, g_id], [("hr", c)])
            wi = nxt("w", NW)
            wrg_sb = wring[wi]
            dma("sp", wrg_sb[:, 0:2048], wb_rg[l, g], [("cw_rg", l, g, gi, n, k) for gi in range(2) for n in range(2) for k in range(2)],
                [("w", wi)], ("w", wi))
            for n in range(2):
                gp = []
                for m in range(2):
                    pr, prid = PS()
                    pi, piid = PS()
                    for gi, (pp_, ppid) in enumerate(((pr, prid), (pi, piid))):
                        for kc in range(2):
                            base = ((gi * 2 + n) * 2 + kc) * 256 + m * 128
                            bsrc = xc_t[n * 2 + kc]
                            mm(pp_[:], wrg_sb[:, base:base + 128], bsrc[2][:], kc == 0, kc == 1,
                               [("w", wi), bsrc[3]], [ppid])
                    gp.append((pr, prid, pi, piid))
                ch = []
                for m in range(2):
                    c = g * 4 + n * 2 + m
                    pr, prid, pi, piid = gp[m]
                    r_t, r_id = FT()
                    i_t, i_id = FT()
                    actf(r_t[:, 0:TB], pr[:], AF.Sigmoid, [prid, "pp"], [r_id], bias=ppc(l, O_BRG, c))
                    actf(i_t[:, 0:TB], pi[:], AF.Sigmoid, [piid, "pp"], [i_id], bias=ppc(l, O_BRG, 16 + c))
                    ch.append((c, r_t, r_id, i_t, i_id))
                ch2 = []
                for (c, r_t, r_id, i_t, i_id) in ch:
                    a_t, a_id = FT()
                    actf(a_t[:, 0:TB], r_t[:, 0:TB], AF.Exp, [r_id, DER], [a_id], scale=der[:, d0 + c:d0 + c + 1])
                    actf(r_t[:, 0:TB], r_t[:, 0:TB], AF.Exp, [r_id, DER], [r_id], scale=der[:, d0 + 16 + c:d0 + 17 + c])
                    ch2.append((c, r_t, r_id, i_t, i_id, a_t, a_id))
                for (c, r_t, r_id, i_t, i_id, a_t, a_id) in ch2:
                    actf(r_t[:, 0:TB], r_t[:, 0:TB], AF.Ln, [r_id], [r_id], scale=-1.0, bias=1.0)
                for (c, r_t, r_id, i_t, i_id, a_t, a_id) in ch2:
                    actf(r_t[:, 0:TB], r_t[:, 0:TB], AF.Exp, [r_id], [r_id], scale=0.5)
                for (c, r_t, r_id, i_t, i_id, a_t, a_id) in ch2:
                    c_t, c_id = xc_t[c - g * 4][0], xc_t[c - g * 4][1]
                    tt("dve", i_t[:, 0:TB], i_t[:, 0:TB], c_t[:, 0:TB], ALU.mult, [i_id, c_id], [i_id])
                    tt("dve", i_t[:, 0:TB], i_t[:, 0:TB], r_t[:, 0:TB], ALU.mult, [i_id, r_id], [i_id])
                    sc.add("dve", lambda e, o=r_t[:, 0:TB], a=a_t[:, 0:TB], u=i_t[:, 0:TB], h0=hstate[:, c:c + 1]:
                           e.tensor_tensor_scan(o, a, u, h0, ALU.mult, ALU.add),
                           reads=[a_id, i_id, ("hs", c)], writes=[r_id])
                    cp("dve", hstate[:, c:c + 1], r_t[:, TB - 1:TB], [r_id], [("hs", c)])
                    tt("dve", hr[c][:], r_t[:, 0:TB], hr[c][:], ALU.mult, [r_id, ("hr", c)], [("hr", c)])

        rope_pend = []

        def rope_flush(keep=0):
            while len(rope_pend) > keep:
                p, pid, b_t, b_id, out_ap, out_id, after = rope_pend.pop(0)
                p2, p2id = PS()
                mm(p2[:], perm[:], b_t[:], True, True, ["perm", b_id], [p2id])
                f1, f1id = FT()
                f2, f2id = FT()
                tt("dve", f1[:, 0:TB], p[:], cos_t[:], ALU.mult, [pid, "cos"], [f1id])
                tt("dve", f2[:, 0:TB], p2[:], sin_t[:], ALU.mult, [p2id, "sin"], [f2id])
                tt("dve", out_ap, f1[:, 0:TB], f2[:, 0:TB], ALU.add, [f1id, f2id], [out_id])
                if after is not None:
                    after()

        def rope_epilogue(p, pid, out_ap, out_id, after=None):
            b_t, b_id = BT()
            cp("act", b_t[:], p[:], [pid], [b_id])
            rope_pend.append((p, pid, b_t, b_id, out_ap, out_id, after))

        def k_group(g):
            sc.tag = "ph3"
            wv, wid = wtile(wb_in[l, :, 2048 + g * 512:2048 + (g + 1) * 512], KC, 512, ("in", l, 2048 + g * 512))
            si = nxt("kst", 1)

            def k_store(g=g, si=si):
                dma("pool", kT_s[l, g * 512:(g + 1) * 512, t0:t0 + TB].rearrange("(c p) t -> p c t", p=128),
                    kst[si][:, :].rearrange("p (c t) -> p c t", t=TB), [("kst", si)], [("kT", l, g, tb)], ("kst", si))

            for j in range(4):
                p, pid = proj_fm(wv, wid, j, xb, xb_ids, KC)
                rope_flush(keep=2)
                rope_epilogue(p, pid, kst[si][:, j * TB:(j + 1) * TB], ("kst", si), after=(k_store if j == 3 else None))

        def v_group(g):
            sc.tag = "ph3v"
            wv, wid = wtile(wb_in[l, :, 4096 + g * 512:4096 + (g + 1) * 512], KC, 512, ("in", l, 4096 + g * 512))
            si = nxt("kst", 1)
            for ts in range(4):
                p, pid = PS()
                for kc in range(KC):
                    mm(p[:], xb[kc][:, ts * 128:(ts + 1) * 128], wv[:, kc, :], kc == 0, kc == KC - 1,
                       [wid, ("xb", kc)], [pid])
                cp("act" if ts % 2 else "dve", kst[si][:, ts * TB:(ts + 1) * TB], p[:], [pid], [("kst", si)])
            dma("pool", v_s[l, t0:t0 + TB, g * 512:(g + 1) * 512].rearrange("(s p) e -> p s e", p=128),
                kst[si][:, :].rearrange("p (s e) -> p s e", e=512), [("kst", si)], [("v", l, g, tb)], ("kst", si))

        def q_group(g):
            sc.tag = "ph4"
            wv, wid = wtile(wb_in[l, :, g * 512:(g + 1) * 512], KC, 512, ("in", l, g * 512))
            for j in range(4):
                c = g * 4 + j
                p, pid = proj_fm(wv, wid, j, xb, xb_ids, KC)
                rope_flush(keep=2)
                rope_epilogue(p, pid, qb[c][:], ("qb", c))

        for g in range(4):
            rnn_group(g)
            if g == 2 and not first:
                load_xf(l, tb)
            k_group(g)
            q_group(g)
            rope_flush()
            v_group(g)

        phase(5)
        S1, S2 = (ps[0], ("ps", 0)), (ps[1], ("ps", 1))
        OA = [[(ps[2], ("ps", 2)), (ps[3], ("ps", 3))], [(ps[4], ("ps", 4)), (ps[5], ("ps", 5))]]
        LS = [(ps[6], ("ps", 6)), (ps[7], ("ps", 7))]
        scale = HD ** -0.5
        nchunks = 4 * (tb + 1)
        kv_tiles = {}

        def load_kv(h, kb):
            i = nxt("kv", NKV)
            dma("pool", Kt[i][:, :].rearrange("p (c t) -> p c t", t=TB),
                kT_s[l, h * 256:(h + 1) * 256, kb * TB:(kb + 1) * TB].rearrange("(c p) t -> p c t", p=128),
                [("kT", l, h // 2, kb)], [("Kt", i)], ("Kt", i))
            dma("pool", Vt[i][:, :].rearrange("p (s e) -> p s e", e=256),
                v_s[l, kb * TB:(kb + 1) * TB, h * 256:(h + 1) * 256].rearrange("(s p) e -> p s e", p=128),
                [("v", l, h // 2, kb)], [("Vt", i)], ("Vt", i))
            kv_tiles[(h, kb)] = i

        def scores(h, j):
            kb, ks = j // 4, j % 4
            if (h, kb) not in kv_tiles:
                load_kv(h, kb)
            i = kv_tiles[(h, kb)]
            q0 = 128 * ks if kb == tb else 0
            for cc, (sp_, spid) in enumerate((S1, S2)):
                mm(sp_[:, q0:TB], Kt[i][:, cc * TB + ks * 128:cc * TB + (ks + 1) * 128], qb[2 * h + cc][:, q0:TB], True, True,
                   [("Kt", i), ("qb", 2 * h + cc)], [spid])
                e_i = (j % 2) * 2 + cc
                actf(Et[e_i][:, q0:TB], sp_[:, q0:TB], AF.Exp, [spid], [("Et", e_i)], scale=scale)
                if kb == tb:
                    tt("dve", Et[e_i][:, q0:TB], Et[e_i][:, q0:TB], masks[:, 384 - 128 * ks + q0:384 - 128 * ks + TB], ALU.mult,
                       [("Et", e_i), "masks"], [("Et", e_i)])

        def av(h, j):
            kb, ks = j // 4, j % 4
            i = kv_tiles[(h, kb)]
            q0 = 128 * ks if kb == tb else 0
            st, sp = (j == 0), (j == nchunks - 1)
            for cc in range(2):
                e_i = (j % 2) * 2 + cc
                for ec in range(2):
                    o_, oid = OA[cc][ec]
                    mm(o_[:, q0:TB], Vt[i][:, ks * 256 + ec * 128:ks * 256 + (ec + 1) * 128], Et[e_i][:, q0:TB], st, sp,
                       [("Vt", i), ("Et", e_i)], [oid])
                l_, lid = LS[cc]
                mm(l_[:, q0:TB], ones_bf[:], Et[e_i][:, q0:TB], st, sp, ["ones_bf", ("Et", e_i)], [lid])

        def epilogue_a(h):
            cpy = [None] * 4
            for ec in range(2):
                c_t, c_id = FT()
                cp("act", c_t[:, 0:TB], OA[0][ec][0][:], [OA[0][ec][1]], [c_id])
                cpy[ec] = (c_t, c_id)
            for ec in range(2):
                c_t, c_id = FT()
                cp("dve", c_t[:, 0:TB], OA[1][ec][0][:], [OA[1][ec][1]], [c_id])
                cpy[2 + ec] = (c_t, c_id)
            rl = []
            for cc in range(2):
                r_t, r_id = FT()
                recip(r_t[:, 0:TB], LS[cc][0][:], [LS[cc][1]], [r_id])
                rl.append((r_t, r_id))
            od = []
            for ec in range(2):
                a_t, a_id = cpy[ec]
                b_t, b_id = cpy[2 + ec]
                tt("dve", a_t[:, 0:TB], a_t[:, 0:TB], rl[0][0][:, 0:TB], ALU.mult, [a_id, rl[0][1]], [a_id])
                tt("dve", b_t[:, 0:TB], b_t[:, 0:TB], rl[1][0][:, 0:TB], ALU.mult, [b_id, rl[1][1]], [b_id])
                stt("dve", a_t[:, 0:TB], b_t[:, 0:TB], der[:, d0 + 34:d0 + 35], a_t[:, 0:TB], ALU.mult, ALU.add,
                    [a_id, b_id, DER], [a_id])
                q_t, q_id = BT()
                tt("dve", q_t[:], a_t[:, 0:TB], a_t[:, 0:TB], ALU.mult, [a_id], [q_id])
                od.append((a_t, a_id, q_t, q_id))
            return od

        def epilogue_b(h, od):
            p, pid = S1
            for ec in range(2):
                mm(p[:], ones_sub[:], od[ec][2][:], ec == 0, ec == 1, ["ones_sub", od[ec][3]], [pid])
            r_t, r_id = FT()
            actf(r_t[:, 0:TB], p[:], AF.Ln, [pid], [r_id], bias=EPS)
            actf(r_t[:, 0:TB], r_t[:, 0:TB], AF.Exp, [r_id], [r_id], scale=-0.5)
            for ec in range(2):
                stt("dve", ob[2 * h + ec][:], od[ec][0][:, 0:TB], der[:, d0 + 32 + ec:d0 + 33 + ec], r_t[:, 0:TB],
                    ALU.mult, ALU.mult, [od[ec][1], r_id, DER], [("ob", 2 * h + ec)])

        pend = None
        jdef = min(3, nchunks - 1)
        scores(0, 0)
        for h in range(NH):
            for j in range(nchunks):
                if j + 1 < nchunks:
                    scores(h, j + 1)
                av(h, j)
                if j == jdef and pend is not None:
                    epilogue_b(*pend)
                    pend = None
            if h + 1 < NH:
                scores(h + 1, 0)
            od = epilogue_a(h)
            pend = (h, od)

        phase(6)
        mg = qb
        ob_ids = [("ob", c) for c in range(KC)]
        hr_ids = [("hr", c) for c in range(KC)]
        for g in range(4):
            gts = []
            for which in range(2):
                wv, wid = wtile(wb_in[l, :, 10240 + which * 2048 + g * 512:10240 + which * 2048 + (g + 1) * 512],
                                KC, 512, ("in", l, 10240 + which * 2048 + g * 512))
                for j in range(4):
                    c = g * 4 + j
                    p, pid = proj_fm(wv, wid, j, xb, xb_ids, KC)
                    g_t, g_id = FT()
                    actf(g_t[:, 0:TB], p[:], AF.Sigmoid, [pid, "pp"], [g_id], bias=ppc(l, O_BM, which * 16 + c))
                    gts.append((g_t, g_id))
                    if g == 0 and which == 0 and j == 1:
                        epilogue_b(*pend)
            wv, wid = wtile(wb_br[l, 0, :, g * 512:(g + 1) * 512], KC, 512, ("br0", l, g * 512))
            for j in range(4):
                p, pid = proj_fm(wv, wid, j, ob, ob_ids, KC)
                g_t, g_id = gts[j]
                tt("dve", g_t[:, 0:TB], g_t[:, 0:TB], p[:], ALU.mult, [g_id, pid], [g_id])
            wv, wid = wtile(wb_br[l, 1, :, g * 512:(g + 1) * 512], KC, 512, ("br1", l, g * 512))
            for j in range(4):
                c = g * 4 + j
                p, pid = proj_fm(wv, wid, j, hr, hr_ids, KC)
                g_t, g_id = gts[4 + j]
                tt("dve", g_t[:, 0:TB], g_t[:, 0:TB], p[:], ALU.mult, [g_id, pid], [g_id])
                tt("dve", mg[c][:], g_t[:, 0:TB], gts[j][0][:, 0:TB], ALU.add, [g_id, gts[j][1]], [("qb", c)])

        def layer_norm(j, make_bf):
            pm, pmid = PS()
            pq, pqid = PS()
            for c in range(KC):
                b1, b1id = BT()
                b2, b2id = BT()
                cp("act", b1[:], xf[c][:], [("xf", c)], [b1id])
                actf(b2[:], xf[c][:], AF.Square, [("xf", c)], [b2id])
                mm(pm[:], ones_ln[:], b1[:], c == 0, c == KC - 1, ["ones_ln", b1id], [pmid])
                mm(pq[:], ones_ln[:], b2[:], c == 0, c == KC - 1, ["ones_ln", b2id], [pqid])
            mu, muid = FT()
            rs, rsid = FT()
            cp("dve", mu[:, 0:TB], pm[:], [pmid], [muid])
            tt("dve", rs[:, 0:TB], mu[:, 0:TB], mu[:, 0:TB], ALU.mult, [muid], [rsid])
            tt("dve", rs[:, 0:TB], pq[:], rs[:, 0:TB], ALU.subtract, [pqid, rsid], [rsid])
            actf(rs[:, 0:TB], rs[:, 0:TB], AF.Ln, [rsid], [rsid], bias=EPS)
            actf(rs[:, 0:TB], rs[:, 0:TB], AF.Exp, [rsid], [rsid], scale=-0.5)
            for c in range(KC):
                tt("dve", xf[c][:], xf[c][:], mu[:, 0:TB], ALU.subtract, [("xf", c), muid], [("xf", c)])
                tt("dve", xf[c][:], xf[c][:], rs[:, 0:TB], ALU.mult, [("xf", c), rsid], [("xf", c)])
                if make_bf:
                    actf(xb[c][:], xf[c][:], AF.Identity, [("xf", c), "pp"], [("xb", c)],
                         bias=ppc(l, O_LNB, j * 16 + c), scale=ppc(l, O_LNG, j * 16 + c))
                actf(xf[c][:], xf[c][:], AF.Identity, [("xf", c), "pp"], [("xf", c)],
                     bias=ppc(l, O_LNB, j * 16 + c), scale=ppc(l, O_LNG, j * 16 + c))


        phase(7)
        mg_ids = [("qb", c) for c in range(KC)]
        for g in range(4):
            wv, wid = wtile(wb_out[l, :, g * 512:(g + 1) * 512], KC, 512, ("out", l, g * 512))
            for j in range(4):
                c = g * 4 + j
                p, pid = proj_fm(wv, wid, j, mg, mg_ids, KC)
                stt("dve", xf[c][:], xf[c][:], ALPHA, p[:], ALU.mult, ALU.add, [("xf", c), pid], [("xf", c)])
        layer_norm(0, True)

        phase(8)
        for g in range(11):
            wv, wid = wtile(wb_gu[l, :, g * 512:(g + 1) * 512], KC, 512, ("gu", l, g * 512))
            sg = []
            for j in range(4):
                p, pid = proj_fm(wv, wid, j, xb, xb_ids, KC)
                s_t, s_id = FT()
                actf(s_t[:, 0:TB], p[:], AF.Silu, [pid], [s_id])
                sg.append((s_t, s_id))
            wv, wid = wtile(wb_gu[l, :, DFF + g * 512:DFF + (g + 1) * 512], KC, 512, ("gu", l, DFF + g * 512))
            for j in range(4):
                c = g * 4 + j
                p, pid = proj_fm(wv, wid, j, xb, xb_ids, KC)
                tt("dve", act[c][:], sg[j][0][:, 0:TB], p[:], ALU.mult, [sg[j][1], pid], [act_ids[c]])
        nb = (l, tb + 1) if tb + 1 < NTB else ((l + 1, 0) if l + 1 < L else None)
        for g in range(8):
            if g == 2 and nb is not None:
                prefetch_xb(*nb)
            pa = [PS(), PS()]
            for half in range(2):
                wv, wid = wtile(wb_dn[l, half * 2816:(half + 1) * 2816, g * 256:(g + 1) * 256], 22, 256, ("dn", l, g * 256), r0=half * 2816)
                for j in range(2):
                    for kc in range(22):
                        k = half * 22 + kc
                        mm(pa[j][0][:], wv[:, kc, j * 128:(j + 1) * 128], act[k][:], k == 0, k == FC - 1,
                           [wid, act_ids[k]], [pa[j][1]])
            for j in range(2):
                c = g * 2 + j
                stt("dve", xf[c][:], xf[c][:], ALPHA, pa[j][0][:], ALU.mult, ALU.add, [("xf", c), pa[j][1]], [("xf", c)])
        layer_norm(1, False)

    convert_layer(0)
    for l in range(L):
        pump(len(pending_cv))
        if l + 1 < L:
            convert_layer(l + 1, defer=True)
        derive(l)
        memset("dve", carry[:], 0.0, [("carry", c) for c in range(KC)])
        memset("dve", hstate[:], 0.0, [("hs", c) for c in range(KC)])
        for tb in range(NTB):
            block(l, tb)
    sc.add("pool", lambda e: e.engine_nop(), reads=[("xs", L - 1, tb, c) for tb in range(NTB) for c in range(KC)],
           writes=["final"])
    sc.emit(nc, stack)
    stack.close()
    nc._sched = sc
    return nc


def _pack_pp(inp, layers):
    cols = []
    for l in layers:
        def fm(a):
            a = np.asarray(a, np.float32).reshape(-1, KC, 128)
            return np.ascontiguousarray(a.transpose(2, 0, 1).reshape(128, -1))
        blk = [fm(inp["b_merge"][l]), fm(inp["conv_w"][l]), fm(inp["conv_b"][l]), fm(inp["b_rg"][l]),
               fm(inp["lru_lambda"][l]), fm(inp["ln_g"][l]), fm(inp["ln_b"][l]),
               np.ascontiguousarray(np.asarray(inp["subln_g"][l], np.float32).reshape(2, 128).T),
               np.ascontiguousarray(np.asarray(inp["diff_lambda"][l], np.float32).T)]
        blk = np.concatenate(blk, axis=1)
        assert blk.shape == (128, NPP), blk.shape
        cols.append(blk)
    return np.ascontiguousarray(np.concatenate(cols, axis=1))


def _consts(S):
    import ml_dtypes
    half = HD // 2
    inv = (10000.0 ** (-np.arange(half, dtype=np.float32) * 2.0 / HD)).astype(np.float32)
    ang = np.arange(S, dtype=np.float32)[:, None] * inv[None, :]
    ang = np.concatenate([ang, ang], axis=-1)
    cosT = np.ascontiguousarray(np.cos(ang).T.astype(np.float32))
    sgn = np.where(np.arange(HD) < half, -1.0, 1.0).astype(np.float32)
    sinS = np.ascontiguousarray((np.sin(ang) * sgn[None, :]).T.astype(np.float32))
    perm = np.zeros((128, 128), np.float32)
    for d2 in range(128):
        perm[(d2 + 64) % 128, d2] = 1.0
    k = np.arange(128)[:, None]
    xx = np.arange(TB + 384)[None, :]
    masks = ((xx - 384) >= k).astype(np.float32)
    return cosT, sinS, perm.astype(ml_dtypes.bfloat16), masks.astype(ml_dtypes.bfloat16)


_PROG_CACHE = {}


def _run(xT_list, inp, layers, S, layer0):
    key = (len(layers), S, layer0)
    if key not in _PROG_CACHE:
        _PROG_CACHE[key] = build_program(len(layers), S, layer0=layer0)
    nc = _PROG_CACHE[key]
    cosT, sinS, perm, masks = _consts(S)
    sl = slice(layers[0], layers[-1] + 1)
    shared = {
        "w_in": np.ascontiguousarray(inp["w_in"][sl]), "w_rg": np.ascontiguousarray(inp["w_rg"][sl]),
        "w_branch": np.ascontiguousarray(inp["w_branch"][sl]), "w_out": np.ascontiguousarray(inp["w_out"][sl]),
        "w_gate_up": np.ascontiguousarray(inp["w_gate_up"][sl]), "w_down": np.ascontiguousarray(inp["w_down"][sl]),
        "pp": _pack_pp(inp, layers), "cosT": cosT, "sinS": sinS, "perm": perm, "masks": masks,
    }
    in_maps = [dict(shared, xT=xT) for xT in xT_list]
    res = run_bass_kernel_spmd(nc, in_maps, core_ids=list(range(len(xT_list))))
    return [np.asarray(r["outT"]) for r in res.results]


def kernel(**inputs):
    inp = {k: np.asarray(v) for k, v in inputs.items()}
    x = inp["x"].astype(np.float32, copy=False)
    B, S, _ = x.shape
    xT = [np.ascontiguousarray(x[b].T) for b in range(B)]
    outT = _run(xT, inp, list(range(DEPTH)), S, 0)
    return np.stack([o.T for o in outT], axis=0).astype(np.float32)
```

```python
import math
import os
from contextlib import ExitStack

import numpy as np
import concourse.bass as bass
import concourse.mybir as mybir
from concourse.bass_utils import run_bass_kernel_spmd

F32 = mybir.dt.float32
BF16 = mybir.dt.bfloat16
AF = mybir.ActivationFunctionType
ALU = mybir.AluOpType

D = 2048
KC = 16
HD = 128
NH = 8
DFF = 5632
FC = 44
N_IN = 14336
TB = 512
EPS = 1e-5
DEPTH = 4
ALPHA = (2.0 * DEPTH) ** 0.25
NPP = 230
O_BM, O_CW, O_CB, O_BRG, O_LAM, O_LNG, O_LNB, O_SUB, O_DL = 0, 32, 96, 112, 144, 160, 192, 224, 226
GELU_C = 2.0 * math.sqrt(2.0 / math.pi)


class _Op:
    __slots__ = ("eng", "fn", "deps", "dom", "sig", "seq", "dma", "tag")

    def __init__(self, eng, fn, deps, dom, dma):
        self.eng, self.fn, self.deps, self.dom, self.dma = eng, fn, deps, dom, dma
        self.sig = False
        self.seq = 0


class Sched:
    def __init__(self):
        self.ops = []
        self.lastw = {}
        self.readers = {}
        self.dma_last = {}
        self.tag = ""
        self.gen = {}

    def _norm(self, bufs):
        out = []
        for b in bufs:
            if isinstance(b, tuple) and len(b) == 3 and b[0] in ("ft", "bt"):
                assert self.gen[b[:2]] == b[2], ("stale temp tile", b, self.gen[b[:2]], self.tag)
                b = b[:2]
            out.append(b)
        return out

    def add(self, eng, fn, reads=(), writes=(), dma_key=None):
        idx = len(self.ops)
        deps = set()
        reads = self._norm(reads)
        writes = self._norm(writes)
        ex = [b for b in reads if isinstance(b, tuple) and b[0] == "ps"]
        if ex:
            reads = [b for b in reads if b not in ex]
            writes = list(writes) + ex
        for b in reads:
            w = self.lastw.get(b)
            if w is not None:
                deps.add(w)
        for b in writes:
            w = self.lastw.get(b)
            if w is not None:
                deps.add(w)
            rd = self.readers.get(b)
            if rd:
                deps.update(rd.values())
        dom = ("dma", dma_key) if dma_key is not None else eng
        if dma_key is not None:
            p = self.dma_last.get(dma_key)
            if p is not None:
                deps.add(p)
            self.dma_last[dma_key] = idx
        for b in reads:
            self.readers.setdefault(b, {})[dom] = idx
        for b in writes:
            self.lastw[b] = idx
            self.readers[b] = {}
        deps.discard(idx)
        if eng == "pe":
            deps = {d for d in deps if self.ops[d].eng != "pe"}
        op = _Op(eng, fn, deps, dom, dma_key is not None)
        op.tag = self.tag
        self.ops.append(op)
        return idx

    def emit(self, nc, stack):
        ops = self.ops
        for op in ops:
            if op.dma:
                op.sig = True
            for d in op.deps:
                ops[d].sig = True
        cnt = {}
        for op in ops:
            if op.sig:
                cnt[op.dom] = cnt.get(op.dom, 0) + 1
                op.seq = cnt[op.dom]
        sems = {}
        for i, dom in enumerate(cnt):
            sems[dom] = stack.enter_context(nc.semaphore("s%d" % i))
        per_eng = {}
        for i, op in enumerate(ops):
            per_eng.setdefault(op.eng, []).append(i)
        block = stack.enter_context(nc.Block())

        def run(eng_name, e):
            known = {}
            for i in per_eng.get(eng_name, []):
                op = ops[i]
                need = {}
                for d in op.deps:
                    o = ops[d]
                    if o.seq > need.get(o.dom, 0):
                        need[o.dom] = o.seq
                for dom, seq in need.items():
                    if known.get(dom, 0) >= seq:
                        continue
                    known[dom] = seq
                    unit = 16 if dom[0] == "dma" else 1
                    e.wait_ge(sems[dom], seq * unit)
                ins = op.fn(e)
                if op.sig:
                    ins.then_inc(sems[op.dom], 16 if op.dma else 1)

        @block.tensor
        def _(e):
            run("pe", e)

        @block.scalar
        def _(e):
            run("act", e)

        @block.vector
        def _(e):
            run("dve", e)

        @block.gpsimd
        def _(e):
            run("pool", e)

        @block.sync
        def _(e):
            run("sp", e)


def build_program(n_layers, S, layer0=0, dbg=False):
    NTB = S // TB
    L = n_layers
    nc = bass.Bass("TRN2", target_bir_lowering=False)
    sc = Sched()
    stack = ExitStack()

    def dram_in(name, shape, dt=F32):
        return nc.dram_tensor(name, list(shape), dt, kind="ExternalInput").ap()

    def dram_tmp(name, shape, dt):
        return nc.dram_tensor(name, list(shape), dt, kind="Internal").ap()

    xT_in = dram_in("xT", [D, S])
    w_in = dram_in("w_in", [L, D, N_IN])
    w_rg = dram_in("w_rg", [L, 2, 8, 256, 256])
    w_br = dram_in("w_branch", [L, 2, D, D])
    w_out = dram_in("w_out", [L, D, D])
    w_gu = dram_in("w_gate_up", [L, D, 2 * DFF])
    w_dn = dram_in("w_down", [L, DFF, D])
    pp_in = dram_in("pp", [128, L * NPP])
    cos_in = dram_in("cosT", [128, S])
    sin_in = dram_in("sinS", [128, S])
    perm_in = dram_in("perm", [128, 128], BF16)
    mask_in = dram_in("masks", [128, TB + 384], BF16)
    outT = nc.dram_tensor("outT", [D, S], F32, kind="ExternalOutput").ap()

    wb_in = dram_tmp("wb_in", [L, D, N_IN], BF16)
    wb_rg = dram_tmp("wb_rg", [L, 4, 128, 2048], BF16)
    wb_br = dram_tmp("wb_br", [L, 2, D, D], BF16)
    wb_out = dram_tmp("wb_out", [L, D, D], BF16)
    wb_gu = dram_tmp("wb_gu", [L, D, 2 * DFF], BF16)
    wb_dn = dram_tmp("wb_dn", [L, DFF, D], BF16)
    kT_s = dram_tmp("kT_s", [L, D, S], BF16)
    v_s = dram_tmp("v_s", [L, S, D], BF16)
    xs = dram_tmp("x_s", [max(L - 1, 1), D, S], F32)

    def sb(name, shape, dt):
        return stack.enter_context(nc.sbuf_tensor(name, list(shape), dt))

    def pst(name):
        return stack.enter_context(nc.psum_tensor(name, [128, 512], F32))

    NW = 3
    wring = [sb("wr%d" % i, [128, 8192], BF16) for i in range(NW)]
    pp = sb("pp_sb", [128, L * NPP], F32)
    der = sb("der", [128, L * 40], F32)
    ones_bf = sb("ones_bf", [128, 128], BF16)
    ones_ln = sb("ones_ln", [128, 128], BF16)
    ones_sub = sb("ones_sub", [128, 128], BF16)
    ones_f = sb("ones_f", [128, 128], F32)
    perm = sb("perm_sb", [128, 128], BF16)
    masks = sb("mask_sb", [128, TB + 384], BF16)
    cos_t = sb("cos_t", [128, TB], F32)
    sin_t = sb("sin_t", [128, TB], F32)
    xf = [sb("xf%d" % c, [128, TB], F32) for c in range(KC)]
    xb = [sb("xb%d" % c, [128, TB], BF16) for c in range(KC)]
    carry = sb("carry", [128, KC * 3], F32)
    hstate = sb("hstate", [128, KC], F32)
    NFT = 9
    ft = [sb("ft%d" % i, [128, 3 + TB], F32) for i in range(NFT)]
    xcp = [sb("xc%d" % i, [128, TB], F32) for i in range(4)]
    NBT = 8
    bt = [sb("bt%d" % i, [128, TB], BF16) for i in range(NBT)]
    hr = [sb("hr%d" % c, [128, TB], BF16) for c in range(KC)]
    qb = [sb("qb%d" % c, [128, TB], BF16) for c in range(KC)]
    ob = [sb("ob%d" % c, [128, TB], BF16) for c in range(KC)]
    kst = [sb("kst%d" % i, [128, 4 * TB], BF16) for i in range(1)]
    NKV = 2
    Kt = [sb("Kt%d" % i, [128, 2 * TB], BF16) for i in range(NKV)]
    Vt = [sb("Vt%d" % i, [128, 4 * 256], BF16) for i in range(NKV)]
    Et = [sb("Et%d" % i, [128, TB], BF16) for i in range(4)]
    act = hr + ob + qb[:FC - 2 * KC]
    act_ids = [("hr", c) for c in range(KC)] + [("ob", c) for c in range(KC)] + [("qb", c) for c in range(FC - 2 * KC)]
    ps = [pst("ps%d" % i) for i in range(8)]

    ctr = {"ft": 0, "bt": 0, "ps": 0, "w": 0, "kv": 0, "kst": 0}

    def nxt(kind, n):
        i = ctr[kind]
        ctr[kind] = (i + 1) % n
        return i

    gens = {"n": 0}

    def FT():
        i = nxt("ft", NFT)
        gens["n"] += 1
        sc.gen[("ft", i)] = gens["n"]
        return ft[i], ("ft", i, gens["n"])

    def BT():
        i = nxt("bt", NBT)
        gens["n"] += 1
        sc.gen[("bt", i)] = gens["n"]
        return bt[i], ("bt", i, gens["n"])

    def PS():
        i = nxt("ps", 8)
        return ps[i], ("ps", i)

    def dma(eng, out, in_, reads, writes, key):
        sc.add(eng, lambda e: e.dma_start(out=out, in_=in_), reads=reads, writes=writes, dma_key=key)

    def mm(out, lhsT, rhs, start, stop, reads, writes):
        sc.add("pe", lambda e: e.matmul(out, lhsT, rhs, start=start, stop=stop), reads=reads, writes=writes)

    def actf(out, in_, func, reads, writes, bias=None, scale=None):
        kw = {}
        if bias is not None:
            kw["bias"] = bias
        if scale is not None:
            kw["scale"] = scale
        sc.add("act", lambda e: e.activation(out, in_, func, **kw), reads=reads, writes=writes)

    def tt(eng, out, in0, in1, op, reads, writes):
        sc.add(eng, lambda e: e.tensor_tensor(out, in0, in1, op), reads=reads, writes=writes)

    def tsc(eng, out, in0, s1, s2, op0, op1, reads, writes):
        if op1 is None:
            sc.add(eng, lambda e: e.tensor_scalar(out, in0, s1, None, op0), reads=reads, writes=writes)
        else:
            sc.add(eng, lambda e: e.tensor_scalar(out, in0, s1, s2, op0, op1), reads=reads, writes=writes)

    def stt(eng, out, in0, scalar, in1, op0, op1, reads, writes):
        sc.add(eng, lambda e: e.scalar_tensor_tensor(out, in0, scalar, in1, op0, op1), reads=reads, writes=writes)

    def cp(eng, out, in_, reads, writes):
        if eng == "act":
            sc.add(eng, lambda e: e.activation(out, in_, AF.Copy), reads=reads, writes=writes)
        else:
            sc.add(eng, lambda e: e.tensor_copy(out, in_), reads=reads, writes=writes)

    def recip(out, in_, reads, writes):
        sc.add("dve", lambda e: e.reciprocal(out, in_), reads=reads, writes=writes)

    def memset(eng, ap, val, writes):
        sc.add(eng, lambda e: e.memset(ap, val), writes=writes)

    memset("dve", ones_bf[:], 1.0, ["ones_bf"])
    memset("dve", ones_ln[:], 1.0 / D, ["ones_ln"])
    memset("dve", ones_sub[:], 1.0 / 256.0, ["ones_sub"])
    memset("dve", ones_f[:], 1.0, ["ones_f"])
    dma("sp", pp[:], pp_in[:, :], [], ["pp"], "c0")
    dma("sp", perm[:], perm_in[:, :], [], ["perm"], "c1")
    dma("sp", masks[:], mask_in[:, :], [], ["masks"], "c2")

    pending_cv = []
    cvn = [0, 0]

    def pump(n):
        for _ in range(n):
            if pending_cv:
                dst, src, wid = pending_cv.pop(0)
                cvn[0] += 1
                dma("pool", dst, src, [], [wid], ("cv", cvn[0] % 8))

    CP = {}

    def convert_layer(l, defer=False):
        RB = 128

        def cv(dst, src, wid):
            pending_cv.append((dst, src, wid))
            if not defer:
                pump(1)

        def conv2d(name, dst, src, nrows, c0, c1, pw):
            pieces = CP.setdefault(name, [])
            for a0 in range(c0, c1, pw):
                a1 = min(a0 + pw, c1)
                if (a0, a1) not in pieces:
                    pieces.append((a0, a1))
                for r0 in range(0, nrows, RB):
                    cv(dst[r0:r0 + RB, a0:a1], src[r0:r0 + RB, a0:a1], ("cw", name, l, r0, a0))

        secs = [(6144, 10240), (2048, 6144), (0, 2048), (10240, 14336)]
        for si, (c0, c1) in enumerate(secs):
            conv2d("in", wb_in[l], w_in[l], D, c0, c1, 2048)
            if si == 0:
                for grp in range(4):
                    for gi in range(2):
                        for n in range(2):
                            for k in range(2):
                                o = ((gi * 2 + n) * 2 + k) * 256
                                cv(wb_rg[l, grp, :, o:o + 256], w_rg[l, gi, grp * 2 + n, k * 128:(k + 1) * 128, :],
                                   ("cw_rg", l, grp, gi, n, k))
        for j in range(2):
            conv2d("br%d" % j, wb_br[l, j], w_br[l, j], D, 0, D, 2048)
        conv2d("out", wb_out[l], w_out[l], D, 0, D, 2048)
        conv2d("gu", wb_gu[l], w_gu[l], D, 0, 2 * DFF, 2816)
        conv2d("dn", wb_dn[l], w_dn[l], DFF, 0, D, 2048)

    def in_sec(c0):
        return 0 if 6144 <= c0 < 10240 else (1 if 2048 <= c0 < 6144 else (2 if c0 < 2048 else 3))

    def wtile(src2d, kcn, ncols, dep, r0=0):
        cvn[1] += 1
        if cvn[1] % 2 == 0:
            pump(1)
        i = nxt("w", NW)
        view = wring[i][:, 0:kcn * ncols].rearrange("p (k n) -> p k n", n=ncols)
        name, l_, c0 = dep
        deps = []
        for (a0, a1) in CP[name]:
            if a0 < c0 + ncols and c0 < a1:
                deps += [("cw", name, l_, r0 + k * 128, a0) for k in range(kcn)]
        dma("sp", view, src2d.rearrange("(k p) n -> p k n", p=128), deps, [("w", i)], ("w", i))
        return view, ("w", i)

    def derive(l):
        lam_init = 0.8 - 0.6 * math.exp(-0.3 * (l + layer0))
        P0 = l * NPP
        d0 = l * 40
        actf(der[:, d0:d0 + 16], pp[:, P0 + O_LAM:P0 + O_LAM + 16], AF.Exp, ["pp"], [("der", l)], scale=-1.0)
        actf(der[:, d0:d0 + 16], der[:, d0:d0 + 16], AF.Ln, [("der", l)], [("der", l)], bias=1.0)
        tsc("dve", der[:, d0 + 16:d0 + 32], der[:, d0:d0 + 16], -16.0, None, ALU.mult, None, [("der", l)], [("der", l)])
        tsc("dve", der[:, d0:d0 + 16], der[:, d0:d0 + 16], -8.0, None, ALU.mult, None, [("der", l)], [("der", l)])
        tsc("dve", der[:, d0 + 32:d0 + 34], pp[:, P0 + O_SUB:P0 + O_SUB + 2], 1.0 - lam_init, None, ALU.mult, None,
            ["pp", ("der", l)], [("der", l)])
        tt("dve", der[:, d0 + 36:d0 + 37], pp[:, P0 + O_DL:P0 + O_DL + 1], pp[:, P0 + O_DL + 1:P0 + O_DL + 2], ALU.mult,
           ["pp", ("der", l)], [("der", l)])
        tt("dve", der[:, d0 + 37:d0 + 38], pp[:, P0 + O_DL + 2:P0 + O_DL + 3], pp[:, P0 + O_DL + 3:P0 + O_DL + 4], ALU.mult,
           ["pp", ("der", l)], [("der", l)])
        p, pid = PS()
        mm(p[:, 0:2], ones_f[:], der[:, d0 + 36:d0 + 38], True, True, ["ones_f", ("der", l)], [pid])
        actf(der[:, d0 + 36:d0 + 38], p[:, 0:2], AF.Exp, [pid], [("der", l)])
        tt("dve", der[:, d0 + 38:d0 + 39], der[:, d0 + 37:d0 + 38], der[:, d0 + 36:d0 + 37], ALU.subtract,
           [("der", l)], [("der", l)])
        tsc("dve", der[:, d0 + 34:d0 + 35], der[:, d0 + 38:d0 + 39], -lam_init, None, ALU.add, None,
            [("der", l)], [("der", l)])

    def ppc(l, off, c):
        return pp[:, l * NPP + off + c:l * NPP + off + c + 1]

    STOP = int(os.environ.get("KSTOP", "99"))

    def block(l, tb):
        try:
            block_body(l, tb)
        except StopIteration:
            pass
        dst = outT if l == L - 1 else xs[l]
        for c in range(KC):
            dma("pool", dst[c * 128:(c + 1) * 128, tb * TB:(tb + 1) * TB], xf[c][:], [("xf", c)], [("xs", l, tb, c)], ("xst", c))

    def phase(k):
        sc.tag = "ph%d" % k
        if STOP < k:
            raise StopIteration

    def x_src(l, tb, c):
        src = xT_in if l == 0 else xs[l - 1]
        return src[c * 128:(c + 1) * 128, tb * TB:(tb + 1) * TB], ("xs", l - 1, tb, c)

    def load_xf(l, tb):
        for c in range(KC):
            s_ap, s_id = x_src(l, tb, c)
            dma("sp", xf[c][:], s_ap, [s_id], [("xf", c)], ("xl", c))

    def prefetch_xb(l, tb):
        for c in range(KC):
            s_ap, s_id = x_src(l, tb, c)
            t_, tid = FT()
            dma("act", t_[:, 0:TB], s_ap, [s_id], [tid], ("xp", c % 8))
            cp("dve", xb[c][:], t_[:, 0:TB], [tid], [("xb", c)])

    def block_body(l, tb):
        t0 = tb * TB
        phase(1)
        last_layer = (l == L - 1)
        DER = ("der", l)
        d0 = l * 40
        first = (l == 0 and tb == 0) or STOP < 8
        if first:
            load_xf(l, tb)
        dma("sp", cos_t[:], cos_in[:, t0:t0 + TB], [], ["cos"], "cos")
        dma("sp", sin_t[:], sin_in[:, t0:t0 + TB], [], ["sin"], "sin")
        if first:
            for c in range(KC):
                cp("act" if c % 2 else "dve", xb[c][:], xf[c][:], [("xf", c)], [("xb", c)])

        xb_ids = [("xb", c) for c in range(KC)]

        def proj_fm(wv, wid, col, rhs, rhs_ids, nk):
            p, pid = PS()
            for kc in range(nk):
                mm(p[:], wv[:, kc, col * 128:(col + 1) * 128], rhs[kc][:], kc == 0, kc == nk - 1,
                   [wid, rhs_ids[kc]], [pid])
            return p, pid

        phase(2)

        def rnn_group(g):
            sc.tag = "ph2"
            wv, wid = wtile(wb_in[l, :, 6144 + g * 512:6144 + (g + 1) * 512], KC, 512, ("in", l, 6144 + g * 512))
            xc_t = []
            for j in range(4):
                c = g * 4 + j
                p, pid = proj_fm(wv, wid, j, xb, xb_ids, KC)
                x_t, x_id = FT()
                cp("dve", x_t[:, 0:3], carry[:, c * 3:c * 3 + 3], [("carry", c)], [x_id])
                actf(x_t[:, 3:3 + TB], p[:], AF.Copy, [pid], [x_id])
                cp("dve", carry[:, c * 3:c * 3 + 3], x_t[:, TB:TB + 3], [x_id], [("carry", c)])
                c_t, c_id = xcp[j], ("xc", j)
                tsc("dve", c_t[:, 0:TB], x_t[:, 0:TB], ppc(l, O_CW, 0 * 16 + c), ppc(l, O_CB, c), ALU.mult, ALU.add,
                    [x_id, "pp"], [c_id])
                for w in range(1, 4):
                    stt("dve", c_t[:, 0:TB], x_t[:, w:w + TB], ppc(l, O_CW, w * 16 + c), c_t[:, 0:TB], ALU.mult, ALU.add,
                        [x_id, c_id, "pp"], [c_id])
                b_t, b_id = BT()
                cp("act", b_t[:], c_t[:, 0:TB], [c_id], [b_id])
                xc_t.append((c_t, c_id, b_t, b_id))
            wv, wid = wtile(wb_in[l, :, 8192 + g * 512:8192 + (g + 1) * 512], KC, 512, ("in", l, 8192 + g * 512))
            for j in range(4):
                c = g * 4 + j
                p, pid = proj_fm(wv, wid, j, xb, xb_ids, KC)
                g_t, g_id = FT()
                a_t, a_id = FT()
                cp("act", g_t[:, 0:TB], p[:], [pid], [g_id])
                actf(a_t[:, 0:TB], g_t[:, 0:TB], AF.Square, [g_id], [a_id])
                tsc("dve", a_t[:, 0:TB], a_t[:, 0:TB], 0.044715, 1.0, ALU.mult, ALU.add, [a_id], [a_id])
                tt("dve", a_t[:, 0:TB], a_t[:, 0:TB], g_t[:, 0:TB], ALU.mult, [a_id, g_id], [a_id])
                actf(a_t[:, 0:TB], a_t[:, 0:TB], AF.Sigmoid, [a_id], [a_id], scale=GELU_C)
                tt("dve", hr[c][:], a_t[:, 0:TB], g_t[:, 0:TB], ALU.mult, [a_id, g_id], [("hr", c)])
            wi = nxt("w", NW)
            wrg_sb = wring[wi]
            dma("sp", wrg_sb[:, 0:2048], wb_rg[l, g], [("cw_rg", l, g, gi, n, k) for gi in range(2) for n in range(2) for k in range(2)],
                [("w", wi)], ("w", wi))
            for n in range(2):
                gp = []
                for m in range(2):
                    pr, prid = PS()
                    pi, piid = PS()
                    for gi, (pp_, ppid) in enumerate(((pr, prid), (pi, piid))):
                        for kc in range(2):
                            base = ((gi * 2 + n) * 2 + kc) * 256 + m * 128
                            bsrc = xc_t[n * 2 + kc]
                            mm(pp_[:], wrg_sb[:, base:base + 128], bsrc[2][:], kc == 0, kc == 1,
                               [("w", wi), bsrc[3]], [ppid])
                    gp.append((pr, prid, pi, piid))
                ch = []
                for m in range(2):
                    c = g * 4 + n * 2 + m
                    pr, prid, pi, piid = gp[m]
                    r_t, r_id = FT()
                    i_t, i_id = FT()
                    actf(r_t[:, 0:TB], pr[:], AF.Sigmoid, [prid, "pp"], [r_id], bias=ppc(l, O_BRG, c))
                    actf(i_t[:, 0:TB], pi[:], AF.Sigmoid, [piid, "pp"], [i_id], bias=ppc(l, O_BRG, 16 + c))
                    ch.append((c, r_t, r_id, i_t, i_id))
                ch2 = []
                for (c, r_t, r_id, i_t, i_id) in ch:
                    a_t, a_id = FT()
                    actf(a_t[:, 0:TB], r_t[:, 0:TB], AF.Exp, [r_id, DER], [a_id], scale=der[:, d0 + c:d0 + c + 1])
                    actf(r_t[:, 0:TB], r_t[:, 0:TB], AF.Exp, [r_id, DER], [r_id], scale=der[:, d0 + 16 + c:d0 + 17 + c])
                    ch2.append((c, r_t, r_id, i_t, i_id, a_t, a_id))
                for (c, r_t, r_id, i_t, i_id, a_t, a_id) in ch2:
                    actf(r_t[:, 0:TB], r_t[:, 0:TB], AF.Ln, [r_id], [r_id], scale=-1.0, bias=1.0)
                for (c, r_t, r_id, i_t, i_id, a_t, a_id) in ch2:
                    actf(r_t[:, 0:TB], r_t[:, 0:TB], AF.Exp, [r_id], [r_id], scale=0.5)
                for (c, r_t, r_id, i_t, i_id, a_t, a_id) in ch2:
                    c_t, c_id = xc_t[c - g * 4][0], xc_t[c - g * 4][1]
                    tt("dve", i_t[:, 0:TB], i_t[:, 0:TB], c_t[:, 0:TB], ALU.mult, [i_id, c_id], [i_id])
                    tt("dve", i_t[:, 0:TB], i_t[:, 0:TB], r_t[:, 0:TB], ALU.mult, [i_id, r_id], [i_id])
                    sc.add("dve", lambda e, o=r_t[:, 0:TB], a=a_t[:, 0:TB], u=i_t[:, 0:TB], h0=hstate[:, c:c + 1]:
                           e.tensor_tensor_scan(o, a, u, h0, ALU.mult, ALU.add),
                           reads=[a_id, i_id, ("hs", c)], writes=[r_id])
                    cp("dve", hstate[:, c:c + 1], r_t[:, TB - 1:TB], [r_id], [("hs", c)])
                    tt("dve", hr[c][:], r_t[:, 0:TB], hr[c][:], ALU.mult, [r_id, ("hr", c)], [("hr", c)])

        rope_pend = []

        def rope_flush(keep=0):
            while len(rope_pend) > keep:
                p, pid, b_t, b_id, out_ap, out_id, after = rope_pend.pop(0)
                p2, p2id = PS()
                mm(p2[:], perm[:], b_t[:], True, True, ["perm", b_id], [p2id])
                f1, f1id = FT()
                f2, f2id = FT()
                tt("dve", f1[:, 0:TB], p[:], cos_t[:], ALU.mult, [pid, "cos"], [f1id])
                tt("dve", f2[:, 0:TB], p2[:], sin_t[:], ALU.mult, [p2id, "sin"], [f2id])
                tt("dve", out_ap, f1[:, 0:TB], f2[:, 0:TB], ALU.add, [f1id, f2id], [out_id])
                if after is not None:
                    after()

        def rope_epilogue(p, pid, out_ap, out_id, after=None):
            b_t, b_id = BT()
            cp("act", b_t[:], p[:], [pid], [b_id])
            rope_pend.append((p, pid, b_t, b_id, out_ap, out_id, after))

        def k_group(g):
            sc.tag = "ph3"
            wv, wid = wtile(wb_in[l, :, 2048 + g * 512:2048 + (g + 1) * 512], KC, 512, ("in", l, 2048 + g * 512))
            si = nxt("kst", 1)

            def k_store(g=g, si=si):
                dma("pool", kT_s[l, g * 512:(g + 1) * 512, t0:t0 + TB].rearrange("(c p) t -> p c t", p=128),
                    kst[si][:, :].rearrange("p (c t) -> p c t", t=TB), [("kst", si)], [("kT", l, g, tb)], ("kst", si))

            for j in range(4):
                p, pid = proj_fm(wv, wid, j, xb, xb_ids, KC)
                rope_flush(keep=2)
                rope_epilogue(p, pid, kst[si][:, j * TB:(j + 1) * TB], ("kst", si), after=(k_store if j == 3 else None))

        def v_group(g):
            sc.tag = "ph3v"
            wv, wid = wtile(wb_in[l, :, 4096 + g * 512:4096 + (g + 1) * 512], KC, 512, ("in", l, 4096 + g * 512))
            si = nxt("kst", 1)
            for ts in range(4):
                p, pid = PS()
                for kc in range(KC):
                    mm(p[:], xb[kc][:, ts * 128:(ts + 1) * 128], wv[:, kc, :], kc == 0, kc == KC - 1,
                       [wid, ("xb", kc)], [pid])
                cp("act" if ts % 2 else "dve", kst[si][:, ts * TB:(ts + 1) * TB], p[:], [pid], [("kst", si)])
            dma("pool", v_s[l, t0:t0 + TB, g * 512:(g + 1) * 512].rearrange("(s p) e -> p s e", p=128),
                kst[si][:, :].rearrange("p (s e) -> p s e", e=512), [("kst", si)], [("v", l, g, tb)], ("kst", si))

        def q_group(g):
            sc.tag = "ph4"
            wv, wid = wtile(wb_in[l, :, g * 512:(g + 1) * 512], KC, 512, ("in", l, g * 512))
            for j in range(4):
                c = g * 4 + j
                p, pid = proj_fm(wv, wid, j, xb, xb_ids, KC)
                rope_flush(keep=2)
                rope_epilogue(p, pid, qb[c][:], ("qb", c))

        for g in range(4):
            rnn_group(g)
            if g == 2 and not first:
                load_xf(l, tb)
            k_group(g)
            q_group(g)
            rope_flush()
            v_group(g)

        phase(5)
        SB = [[(ps[0], ("ps", 0)), (ps[1], ("ps", 1))], [(ps[6], ("ps", 6)), (ps[7], ("ps", 7))]]
        OA = [[(ps[2], ("ps", 2)), (ps[3], ("ps", 3))], [(ps[4], ("ps", 4)), (ps[5], ("ps", 5))]]
        scale = HD ** -0.5
        nchunks = 4 * (tb + 1)
        kv_tiles = {}
        accs = {}

        def load_kv(h, kb):
            i = nxt("kv", NKV)
            dma("pool", Kt[i][:, :].rearrange("p (c t) -> p c t", t=TB),
                kT_s[l, h * 256:(h + 1) * 256, kb * TB:(kb + 1) * TB].rearrange("(c p) t -> p c t", p=128),
                [("kT", l, h // 2, kb)], [("Kt", i)], ("Kt", i))
            dma("pool", Vt[i][:, :].rearrange("p (s e) -> p s e", e=256),
                v_s[l, kb * TB:(kb + 1) * TB, h * 256:(h + 1) * 256].rearrange("(s p) e -> p s e", p=128),
                [("v", l, h // 2, kb)], [("Vt", i)], ("Vt", i))
            kv_tiles[(h, kb)] = i

        def scores(h, j):
            kb, ks = j // 4, j % 4
            if (h, kb) not in kv_tiles:
                load_kv(h, kb)
            if j == 0:
                accs[h] = [(xcp[2 * (h % 2) + cc], ("xc", 2 * (h % 2) + cc)) for cc in range(2)]
            i = kv_tiles[(h, kb)]
            q0 = 128 * ks if kb == tb else 0
            for cc, (sp_, spid) in enumerate(SB[j % 2]):
                mm(sp_[:, q0:TB], Kt[i][:, cc * TB + ks * 128:cc * TB + (ks + 1) * 128], qb[2 * h + cc][:, q0:TB], True, True,
                   [("Kt", i), ("qb", 2 * h + cc)], [spid])
                e_i = (j % 2) * 2 + cc
                actf(Et[e_i][:, q0:TB], sp_[:, q0:TB], AF.Exp, [spid], [("Et", e_i)], scale=scale)
                if kb == tb:
                    tt("dve", Et[e_i][:, q0:TB], Et[e_i][:, q0:TB], masks[:, 384 - 128 * ks + q0:384 - 128 * ks + TB], ALU.mult,
                       [("Et", e_i), "masks"], [("Et", e_i)])
                ac, acid = accs[h][cc]
                if j == 0:
                    cp("dve", ac[:, 0:TB], Et[e_i][:], [("Et", e_i)], [acid])
                else:
                    tt("dve", ac[:, q0:TB], ac[:, q0:TB], Et[e_i][:, q0:TB], ALU.add, [acid, ("Et", e_i)], [acid])

        def av(h, j):
            kb, ks = j // 4, j % 4
            i = kv_tiles[(h, kb)]
            q0 = 128 * ks if kb == tb else 0
            st, sp = (j == 0), (j == nchunks - 1)
            for cc in range(2):
                e_i = (j % 2) * 2 + cc
                for ec in range(2):
                    o_, oid = OA[cc][ec]
                    mm(o_[:, q0:TB], Vt[i][:, ks * 256 + ec * 128:ks * 256 + (ec + 1) * 128], Et[e_i][:, q0:TB], st, sp,
                       [("Vt", i), ("Et", e_i)], [oid])

        def epilogue_a(h):
            LSB = SB[1]
            for cc in range(2):
                ac, acid = accs[h][cc]
                mm(LSB[cc][0][:], ones_f[:], ac[:, 0:TB], True, True, ["ones_f", acid], [LSB[cc][1]])
            cpy = [None] * 4
            for ec in range(2):
                c_t, c_id = FT()
                cp("act", c_t[:, 0:TB], OA[0][ec][0][:], [OA[0][ec][1]], [c_id])
                cpy[ec] = (c_t, c_id)
            for ec in range(2):
                c_t, c_id = FT()
                cp("dve", c_t[:, 0:TB], OA[1][ec][0][:], [OA[1][ec][1]], [c_id])
                cpy[2 + ec] = (c_t, c_id)
            rl = []
            for cc in range(2):
                r_t, r_id = FT()
                recip(r_t[:, 0:TB], LSB[cc][0][:], [LSB[cc][1]], [r_id])
                rl.append((r_t, r_id))
            od = []
            for ec in range(2):
                a_t, a_id = cpy[ec]
                b_t, b_id = cpy[2 + ec]
                tt("dve", a_t[:, 0:TB], a_t[:, 0:TB], rl[0][0][:, 0:TB], ALU.mult, [a_id, rl[0][1]], [a_id])
                tt("dve", b_t[:, 0:TB], b_t[:, 0:TB], rl[1][0][:, 0:TB], ALU.mult, [b_id, rl[1][1]], [b_id])
                stt("dve", a_t[:, 0:TB], b_t[:, 0:TB], der[:, d0 + 34:d0 + 35], a_t[:, 0:TB], ALU.mult, ALU.add,
                    [a_id, b_id, DER], [a_id])
                q_t, q_id = BT()
                tt("dve", q_t[:], a_t[:, 0:TB], a_t[:, 0:TB], ALU.mult, [a_id], [q_id])
                od.append((a_t, a_id, q_t, q_id))
            return od

        def epilogue_b(h, od):
            p, pid = SB[1][0]
            for ec in range(2):
                mm(p[:], ones_sub[:], od[ec][2][:], ec == 0, ec == 1, ["ones_sub", od[ec][3]], [pid])
            r_t, r_id = FT()
            actf(r_t[:, 0:TB], p[:], AF.Ln, [pid], [r_id], bias=EPS)
            actf(r_t[:, 0:TB], r_t[:, 0:TB], AF.Exp, [r_id], [r_id], scale=-0.5)
            for ec in range(2):
                stt("dve", ob[2 * h + ec][:], od[ec][0][:, 0:TB], der[:, d0 + 32 + ec:d0 + 33 + ec], r_t[:, 0:TB],
                    ALU.mult, ALU.mult, [od[ec][1], r_id, DER], [("ob", 2 * h + ec)])

        pend = None
        jdef = min(3, nchunks - 1)
        scores(0, 0)
        for h in range(NH):
            for j in range(nchunks):
                if j + 1 < nchunks:
                    scores(h, j + 1)
                av(h, j)
                if j == jdef and pend is not None:
                    epilogue_b(*pend)
                    pend = None
            if h + 1 < NH:
                scores(h + 1, 0)
            od = epilogue_a(h)
            pend = (h, od)

        phase(6)
        mg = qb
        ob_ids = [("ob", c) for c in range(KC)]
        hr_ids = [("hr", c) for c in range(KC)]
        for g in range(4):
            gts = []
            for which in range(2):
                wv, wid = wtile(wb_in[l, :, 10240 + which * 2048 + g * 512:10240 + which * 2048 + (g + 1) * 512],
                                KC, 512, ("in", l, 10240 + which * 2048 + g * 512))
                for j in range(4):
                    c = g * 4 + j
                    p, pid = proj_fm(wv, wid, j, xb, xb_ids, KC)
                    g_t, g_id = FT()
                    actf(g_t[:, 0:TB], p[:], AF.Sigmoid, [pid, "pp"], [g_id], bias=ppc(l, O_BM, which * 16 + c))
                    gts.append((g_t, g_id))
                    if g == 0 and which == 0 and j == 1:
                        epilogue_b(*pend)
            wv, wid = wtile(wb_br[l, 0, :, g * 512:(g + 1) * 512], KC, 512, ("br0", l, g * 512))
            for j in range(4):
                p, pid = proj_fm(wv, wid, j, ob, ob_ids, KC)
                g_t, g_id = gts[j]
                tt("dve", g_t[:, 0:TB], g_t[:, 0:TB], p[:], ALU.mult, [g_id, pid], [g_id])
            wv, wid = wtile(wb_br[l, 1, :, g * 512:(g + 1) * 512], KC, 512, ("br1", l, g * 512))
            for j in range(4):
                c = g * 4 + j
                p, pid = proj_fm(wv, wid, j, hr, hr_ids, KC)
                g_t, g_id = gts[4 + j]
                tt("dve", g_t[:, 0:TB], g_t[:, 0:TB], p[:], ALU.mult, [g_id, pid], [g_id])
                tt("dve", mg[c][:], g_t[:, 0:TB], gts[j][0][:, 0:TB], ALU.add, [g_id, gts[j][1]], [("qb", c)])

        def layer_norm(j, make_bf):
            pm, pmid = PS()
            pq, pqid = PS()
            for c in range(KC):
                b1, b1id = BT()
                b2, b2id = BT()
                cp("act", b1[:], xf[c][:], [("xf", c)], [b1id])
                actf(b2[:], xf[c][:], AF.Square, [("xf", c)], [b2id])
                mm(pm[:], ones_ln[:], b1[:], c == 0, c == KC - 1, ["ones_ln", b1id], [pmid])
                mm(pq[:], ones_ln[:], b2[:], c == 0, c == KC - 1, ["ones_ln", b2id], [pqid])
            mu, muid = FT()
            rs, rsid = FT()
            cp("dve", mu[:, 0:TB], pm[:], [pmid], [muid])
            tt("dve", rs[:, 0:TB], mu[:, 0:TB], mu[:, 0:TB], ALU.mult, [muid], [rsid])
            tt("dve", rs[:, 0:TB], pq[:], rs[:, 0:TB], ALU.subtract, [pqid, rsid], [rsid])
            actf(rs[:, 0:TB], rs[:, 0:TB], AF.Ln, [rsid], [rsid], bias=EPS)
            actf(rs[:, 0:TB], rs[:, 0:TB], AF.Exp, [rsid], [rsid], scale=-0.5)
            for c in range(KC):
                tt("dve", xf[c][:], xf[c][:], mu[:, 0:TB], ALU.subtract, [("xf", c), muid], [("xf", c)])
                tt("dve", xf[c][:], xf[c][:], rs[:, 0:TB], ALU.mult, [("xf", c), rsid], [("xf", c)])
                if make_bf:
                    actf(xb[c][:], xf[c][:], AF.Identity, [("xf", c), "pp"], [("xb", c)],
                         bias=ppc(l, O_LNB, j * 16 + c), scale=ppc(l, O_LNG, j * 16 + c))
                actf(xf[c][:], xf[c][:], AF.Identity, [("xf", c), "pp"], [("xf", c)],
                     bias=ppc(l, O_LNB, j * 16 + c), scale=ppc(l, O_LNG, j * 16 + c))


        phase(7)
        mg_ids = [("qb", c) for c in range(KC)]
        for g in range(4):
            wv, wid = wtile(wb_out[l, :, g * 512:(g + 1) * 512], KC, 512, ("out", l, g * 512))
            for j in range(4):
                c = g * 4 + j
                p, pid = proj_fm(wv, wid, j, mg, mg_ids, KC)
                stt("dve", xf[c][:], xf[c][:], ALPHA, p[:], ALU.mult, ALU.add, [("xf", c), pid], [("xf", c)])
        layer_norm(0, True)

        phase(8)
        for g in range(11):
            wv, wid = wtile(wb_gu[l, :, g * 512:(g + 1) * 512], KC, 512, ("gu", l, g * 512))
            sg = []
            for j in range(4):
                p, pid = proj_fm(wv, wid, j, xb, xb_ids, KC)
                s_t, s_id = FT()
                actf(s_t[:, 0:TB], p[:], AF.Silu, [pid], [s_id])
                sg.append((s_t, s_id))
            wv, wid = wtile(wb_gu[l, :, DFF + g * 512:DFF + (g + 1) * 512], KC, 512, ("gu", l, DFF + g * 512))
            for j in range(4):
                c = g * 4 + j
                p, pid = proj_fm(wv, wid, j, xb, xb_ids, KC)
                tt("dve", act[c][:], sg[j][0][:, 0:TB], p[:], ALU.mult, [sg[j][1], pid], [act_ids[c]])
        nb = (l, tb + 1) if tb + 1 < NTB else ((l + 1, 0) if l + 1 < L else None)
        for g in range(8):
            if g == 2 and nb is not None:
                prefetch_xb(*nb)
            pa = [PS(), PS()]
            for half in range(2):
                wv, wid = wtile(wb_dn[l, half * 2816:(half + 1) * 2816, g * 256:(g + 1) * 256], 22, 256, ("dn", l, g * 256), r0=half * 2816)
                for j in range(2):
                    for kc in range(22):
                        k = half * 22 + kc
                        mm(pa[j][0][:], wv[:, kc, j * 128:(j + 1) * 128], act[k][:], k == 0, k == FC - 1,
                           [wid, act_ids[k]], [pa[j][1]])
            for j in range(2):
                c = g * 2 + j
                stt("dve", xf[c][:], xf[c][:], ALPHA, pa[j][0][:], ALU.mult, ALU.add, [("xf", c), pa[j][1]], [("xf", c)])
        layer_norm(1, False)

    convert_layer(0)
    for l in range(L):
        pump(len(pending_cv))
        if l + 1 < L:
            convert_layer(l + 1, defer=True)
        derive(l)
        memset("dve", carry[:], 0.0, [("carry", c) for c in range(KC)])
        memset("dve", hstate[:], 0.0, [("hs", c) for c in range(KC)])
        for tb in range(NTB):
            block(l, tb)
    sc.add("pool", lambda e: e.engine_nop(), reads=[("xs", L - 1, tb, c) for tb in range(NTB) for c in range(KC)],
           writes=["final"])
    sc.emit(nc, stack)
    stack.close()
    nc._sched = sc
    return nc


def _pack_pp(inp, layers):
    cols = []
    for l in layers:
        def fm(a):
            a = np.asarray(a, np.float32).reshape(-1, KC, 128)
            return np.ascontiguousarray(a.transpose(2, 0, 1).reshape(128, -1))
        blk = [fm(inp["b_merge"][l]), fm(inp["conv_w"][l]), fm(inp["conv_b"][l]), fm(inp["b_rg"][l]),
               fm(inp["lru_lambda"][l]), fm(inp["ln_g"][l]), fm(inp["ln_b"][l]),
               np.ascontiguousarray(np.asarray(inp["subln_g"][l], np.float32).reshape(2, 128).T),
               np.ascontiguousarray(np.asarray(inp["diff_lambda"][l], np.float32).T)]
        blk = np.concatenate(blk, axis=1)
        assert blk.shape == (128, NPP), blk.shape
        cols.append(blk)
    return np.ascontiguousarray(np.concatenate(cols, axis=1))


def _consts(S):
    import ml_dtypes
    half = HD // 2
    inv = (10000.0 ** (-np.arange(half, dtype=np.float32) * 2.0 / HD)).astype(np.float32)
    ang = np.arange(S, dtype=np.float32)[:, None] * inv[None, :]
    ang = np.concatenate([ang, ang], axis=-1)
    cosT = np.ascontiguousarray(np.cos(ang).T.astype(np.float32))
    sgn = np.where(np.arange(HD) < half, -1.0, 1.0).astype(np.float32)
    sinS = np.ascontiguousarray((np.sin(ang) * sgn[None, :]).T.astype(np.float32))
    perm = np.zeros((128, 128), np.float32)
    for d2 in range(128):
        perm[(d2 + 64) % 128, d2] = 1.0
    k = np.arange(128)[:, None]
    xx = np.arange(TB + 384)[None, :]
    masks = ((xx - 384) >= k).astype(np.float32)
    return cosT, sinS, perm.astype(ml_dtypes.bfloat16), masks.astype(ml_dtypes.bfloat16)


_PROG_CACHE = {}


def _run(xT_list, inp, layers, S, layer0):
    key = (len(layers), S, layer0)
    if key not in _PROG_CACHE:
        _PROG_CACHE[key] = build_program(len(layers), S, layer0=layer0)
    nc = _PROG_CACHE[key]
    cosT, sinS, perm, masks = _consts(S)
    sl = slice(layers[0], layers[-1] + 1)
    shared = {
        "w_in": np.ascontiguousarray(inp["w_in"][sl]), "w_rg": np.ascontiguousarray(inp["w_rg"][sl]),
        "w_branch": np.ascontiguousarray(inp["w_branch"][sl]), "w_out": np.ascontiguousarray(inp["w_out"][sl]),
        "w_gate_up": np.ascontiguousarray(inp["w_gate_up"][sl]), "w_down": np.ascontiguousarray(inp["w_down"][sl]),
        "pp": _pack_pp(inp, layers), "cosT": cosT, "sinS": sinS, "perm": perm, "masks": masks,
    }
    in_maps = [dict(shared, xT=xT) for xT in xT_list]
    res = run_bass_kernel_spmd(nc, in_maps, core_ids=list(range(len(xT_list))))
    return [np.asarray(r["outT"]) for r in res.results]


def kernel(**inputs):
    inp = {k: np.asarray(v) for k, v in inputs.items()}
    x = inp["x"].astype(np.float32, copy=False)
    B, S, _ = x.shape
    xT = [np.ascontiguousarray(x[b].T) for b in range(B)]
    outT = _run(xT, inp, list(range(DEPTH)), S, 0)
    return np.stack([o.T for o in outT], axis=0).astype(np.float32)
```
